# Optimizing a Trainium2 kernel written in Bass

```python
import jax, jax.numpy as jnp
from jax import lax
import numpy as np

D_MODEL = 1024
BATCH = 4
SEQ = 4096
DEPTH = 4

N_MIXERS = 3
PLE_DIM = 256
D_FF = 4 * D_MODEL
RMS_EPS = 1e-6
SC_WIDTH = 3
ATTN_HEAD_DIM = 128
ATTN_HEADS_PER_GROUP = D_MODEL // ATTN_HEAD_DIM
DILATED_PATTERNS = ((128, 1), (512, 4), (2048, 16))
N_ATTN_GROUPS = len(DILATED_PATTERNS)
ROPE_THETA = 500000.0
ROPE_DIM = ATTN_HEAD_DIM // 4
D_RNN = 1280
N_LRU_BLOCKS = 10
LRU_BLOCK = D_RNN // N_LRU_BLOCKS
LRU_CONV_WIDTH = 4
LRU_C = 8.0

N_LAYERS_A = len(range(0, DEPTH, N_MIXERS))
N_LAYERS_B = len(range(1, DEPTH, N_MIXERS))
N_LAYERS_C = len(range(2, DEPTH, N_MIXERS))

kernel_name = 'hybrid_conv_dilattn_rglru_trunk'


def rmsnorm(x, g):
    xf = x.astype(jnp.float32)
    y = xf * lax.rsqrt(jnp.mean(xf * xf, axis=-1, keepdims=True) + RMS_EPS)
    return (y * g.astype(jnp.float32)).astype(x.dtype)


def causal_depthwise_conv(x, w):
    k_width, ch = w.shape
    return lax.conv_general_dilated(
        x, w[:, None, :].astype(x.dtype), window_strides=(1,),
        padding=[(k_width - 1, 0)], dimension_numbers=('NWC', 'WIO', 'NWC'),
        feature_group_count=ch)


def partial_rotary(x, positions):
    half = ROPE_DIM // 2
    inv_freq = ROPE_THETA ** (-2.0 * jnp.arange(half, dtype=jnp.float32) / ROPE_DIM)
    ang = positions.astype(jnp.float32)[..., None] * inv_freq
    cos = jnp.cos(ang)[:, :, None, :]
    sin = jnp.sin(ang)[:, :, None, :]
    xf = x.astype(jnp.float32)
    x1 = xf[..., :half]
    x2 = xf[..., half:ROPE_DIM]
    out = jnp.concatenate([x1 * cos - x2 * sin, x2 * cos + x1 * sin, xf[..., ROPE_DIM:]], axis=-1)
    return out.astype(x.dtype)


def dilated_window_attention(q, k, v, window, dilation):
    b_, s_, h_, dh = q.shape
    n_back = window // dilation
    blk = n_back
    sub_len = s_ // dilation
    n_blk = -(-sub_len // blk)
    sub_pad = n_blk * blk

    def to_sub(t):
        t = t.reshape(b_, sub_len, dilation, h_, dh).transpose(0, 2, 1, 3, 4)
        t = t.reshape(b_ * dilation, sub_len, h_, dh)
        t = jnp.pad(t, ((0, 0), (0, sub_pad - sub_len), (0, 0), (0, 0)))
        return t.reshape(b_ * dilation, n_blk, blk, h_, dh)

    def with_prev(t):
        prev = jnp.pad(t[:, :-1], ((0, 0), (1, 0), (0, 0), (0, 0), (0, 0)))
        return jnp.concatenate([prev, t], axis=2)

    qb = to_sub(q)
    kk = with_prev(to_sub(k))
    vv = with_prev(to_sub(v))
    s = jnp.einsum('nbqhd,nbkhd->nbhqk', qb, kk,
                   preferred_element_type=jnp.float32) * (dh ** -0.5)
    qi = jnp.arange(blk)[:, None]
    kj = jnp.arange(2 * blk)[None, :]
    rel = blk + qi - kj
    band = (rel >= 0) & (rel <= n_back)
    k_pos = jnp.arange(n_blk)[:, None, None] * blk - blk + kj[None]
    valid = band[None] & (k_pos >= 0)
    s = jnp.where(valid[None, :, None], s, -jnp.inf)
    lse = jax.nn.logsumexp(s, axis=-1)
    probs = jnp.exp(s - lse[..., None])
    o = jnp.einsum('nbhqk,nbkhd->nbqhd', probs, vv.astype(jnp.float32))

    def from_sub(t):
        t = t.reshape((b_, dilation, sub_pad) + t.shape[3:])[:, :, :sub_len]
        t = jnp.moveaxis(t, 1, 2)
        return t.reshape((b_, s_) + t.shape[3:])

    return from_sub(o), from_sub(jnp.swapaxes(lse, 2, 3))


def short_conv_mixer(h, w_in, w_conv, w_out):
    gate_b, gate_c, xin = jnp.split(h @ w_in, 3, axis=-1)
    y = gate_b * causal_depthwise_conv(gate_c * xin, w_conv)
    return y @ w_out


def dilated_attention_mixer(h, positions, w_qkv, w_o):
    b_, s_, _ = h.shape
    g_, hh, dh = N_ATTN_GROUPS, ATTN_HEADS_PER_GROUP, ATTN_HEAD_DIM
    qkv = (h @ w_qkv).reshape(b_, s_, 3, g_ * hh, dh)
    q = partial_rotary(qkv[:, :, 0], positions).reshape(b_, s_, g_, hh, dh)
    k = partial_rotary(qkv[:, :, 1], positions).reshape(b_, s_, g_, hh, dh)
    v = qkv[:, :, 2].reshape(b_, s_, g_, hh, dh)
    outs, lses = [], []
    for g, (window, dilation) in enumerate(DILATED_PATTERNS):
        o_g, lse_g = dilated_window_attention(q[:, :, g], k[:, :, g], v[:, :, g], window, dilation)
        outs.append(o_g)
        lses.append(lse_g)
    weights = jax.nn.softmax(jnp.stack(lses, axis=0), axis=0)
    o = jnp.sum(weights[..., None] * jnp.stack(outs, axis=0), axis=0)
    return o.reshape(b_, s_, hh * dh).astype(h.dtype) @ w_o


def rglru_mixer(h, w_in, conv_w, conv_b, w_a, b_a, w_x, b_x, lam, w_out):
    b_, s_, _ = h.shape
    gate, xr = jnp.split(h @ w_in, 2, axis=-1)
    xr = causal_depthwise_conv(xr, conv_w) + conv_b
    xb = xr.reshape(b_, s_, N_LRU_BLOCKS, LRU_BLOCK)
    r_gate = jax.nn.sigmoid(jnp.einsum('bsnj,njk->bsnk', xb, w_a).reshape(b_, s_, D_RNN) + b_a)
    i_gate = jax.nn.sigmoid(jnp.einsum('bsnj,njk->bsnk', xb, w_x).reshape(b_, s_, D_RNN) + b_x)
    log_a = -LRU_C * r_gate.astype(jnp.float32) * jax.nn.softplus(-lam.astype(jnp.float32))
    a = jnp.exp(log_a)
    mult = jnp.sqrt(-jnp.expm1(2.0 * log_a))
    u = mult * (i_gate * xr).astype(jnp.float32)

    def combine(left, right):
        a_l, u_l = left
        a_r, u_r = right
        return a_l * a_r, a_r * u_l + u_r

    _, hs = lax.associative_scan(combine, (a, u), axis=1)
    y = hs.astype(h.dtype) * jax.nn.gelu(gate)
    return y @ w_out


def squared_relu_mlp(h, w_up, w_down):
    return jnp.square(jax.nn.relu(h @ w_up)) @ w_down


def setup_inputs(seed: int = 0) -> dict:
    key = jax.random.key(seed)
    ks = iter(jax.random.split(key, 40))

    def nrm(shape, fan_in):
        return jax.random.normal(next(ks), shape, jnp.float32) * (fan_in ** -0.5)

    def gain(shape):
        return 1.0 + 0.05 * jax.random.normal(next(ks), shape, jnp.float32)

    def bias(shape):
        return 0.1 * jax.random.normal(next(ks), shape, jnp.float32)

    attn_width = N_ATTN_GROUPS * ATTN_HEADS_PER_GROUP * ATTN_HEAD_DIM
    u = jax.random.uniform(next(ks), (N_LAYERS_C, D_RNN), jnp.float32, 0.9, 0.999)
    sig = u ** (1.0 / LRU_C)
    return {
        'x': jax.random.normal(next(ks), (BATCH, SEQ, D_MODEL), jnp.float32),
        'p': jax.random.normal(next(ks), (DEPTH, BATCH, SEQ, PLE_DIM), jnp.float32),
        'positions': (jnp.arange(SEQ, dtype=jnp.int32)[None, :]
                      + jax.random.randint(next(ks), (BATCH, 1), 0, 1024, jnp.int32)),
        'norm_mix': gain((DEPTH, D_MODEL)),
        'norm_mlp': gain((DEPTH, D_MODEL)),
        'norm_ple': gain((DEPTH, D_MODEL)),
        'norm_final': gain((D_MODEL,)),
        'sc_w_in': nrm((N_LAYERS_A, D_MODEL, 3 * D_MODEL), D_MODEL),
        'sc_w_conv': nrm((N_LAYERS_A, SC_WIDTH, D_MODEL), SC_WIDTH),
        'sc_w_out': nrm((N_LAYERS_A, D_MODEL, D_MODEL), D_MODEL),
        'attn_w_qkv': nrm((N_LAYERS_B, D_MODEL, 3 * attn_width), D_MODEL),
        'attn_w_o': nrm((N_LAYERS_B, ATTN_HEADS_PER_GROUP * ATTN_HEAD_DIM, D_MODEL),
                        ATTN_HEADS_PER_GROUP * ATTN_HEAD_DIM),
        'lru_w_in': nrm((N_LAYERS_C, D_MODEL, 2 * D_RNN), D_MODEL),
        'lru_conv_w': nrm((N_LAYERS_C, LRU_CONV_WIDTH, D_RNN), LRU_CONV_WIDTH),
        'lru_conv_b': bias((N_LAYERS_C, D_RNN)),
        'lru_w_a': nrm((N_LAYERS_C, N_LRU_BLOCKS, LRU_BLOCK, LRU_BLOCK), LRU_BLOCK),
        'lru_b_a': bias((N_LAYERS_C, D_RNN)),
        'lru_w_x': nrm((N_LAYERS_C, N_LRU_BLOCKS, LRU_BLOCK, LRU_BLOCK), LRU_BLOCK),
        'lru_b_x': bias((N_LAYERS_C, D_RNN)),
        'lru_lambda': jnp.log(sig) - jnp.log1p(-sig),
        'lru_w_out': nrm((N_LAYERS_C, D_RNN, D_MODEL), D_RNN),
        'mlp_w_up': nrm((DEPTH, D_MODEL, D_FF), D_MODEL),
        'mlp_w_down': nrm((DEPTH, D_FF, D_MODEL), D_FF),
        'ple_w_gate': nrm((DEPTH, D_MODEL, D_MODEL), D_MODEL),
        'ple_w_proj': nrm((DEPTH, PLE_DIM, D_MODEL), PLE_DIM),
    }


def reference(x, p, positions, norm_mix, norm_mlp, norm_ple, norm_final,
              sc_w_in, sc_w_conv, sc_w_out, attn_w_qkv, attn_w_o,
              lru_w_in, lru_conv_w, lru_conv_b, lru_w_a, lru_b_a, lru_w_x, lru_b_x,
              lru_lambda, lru_w_out, mlp_w_up, mlp_w_down, ple_w_gate, ple_w_proj):
    h = x
    for i in range(DEPTH):
        kind, j = i % N_MIXERS, i // N_MIXERS
        hn = rmsnorm(h, norm_mix[i])
        if kind == 0:
            mixed = short_conv_mixer(hn, sc_w_in[j], sc_w_conv[j], sc_w_out[j])
        elif kind == 1:
            mixed = dilated_attention_mixer(hn, positions, attn_w_qkv[j], attn_w_o[j])
        else:
            mixed = rglru_mixer(hn, lru_w_in[j], lru_conv_w[j], lru_conv_b[j], lru_w_a[j],
                                lru_b_a[j], lru_w_x[j], lru_b_x[j], lru_lambda[j], lru_w_out[j])
        h = h + mixed
        h = h + squared_relu_mlp(rmsnorm(h, norm_mlp[i]), mlp_w_up[i], mlp_w_down[i])
        ple_gate = jax.nn.sigmoid(rmsnorm(h, norm_ple[i]) @ ple_w_gate[i])
        h = h + ple_gate * (p[i].astype(h.dtype) @ ple_w_proj[i])
    return rmsnorm(h, norm_final)
```

```python
import numpy as np
from contextlib import ExitStack
import concourse.bass as bass
import concourse.mybir as mybir
from concourse.bass_utils import run_bass_kernel_spmd

F32, BF16, I32 = mybir.dt.float32, mybir.dt.bfloat16, mybir.dt.int32
AF = mybir.ActivationFunctionType
ALU = mybir.AluOpType

T = 2048
TT = 512
NT = T // TT
D = 1024
KC = 8
DEPTH = 4
EPS = 1e-6
N_CORES = 8
PAIRS = [[0, 1], [2, 3], [4, 5], [6, 7]]

C_NMIX = 0
C_NMLP = 32
C_NPLE = 64
C_NFIN = 96
C_SCW = 104
C_LCW = 152
C_LCB = 192
C_LBA = 202
C_LBX = 212
C_LAM = 222
C_FLAG = 232
C_INVF = 233
NCF = 234
M_ID = 0
M_ONESD = 128
M_ONES = 256
M_ROT = 384
M_MASK = 416
M_MASKH = 672
M_ROT128 = 928
NCM = 1056

S_BYTES = 50176
MIXER_ONLY = False
ATT_STAGE = 9


class Sched:
    ENG = ("pe", "act", "dve", "pool", "sp")

    def __init__(self, nc, es):
        self.nc, self.es = nc, es
        self.streams = {e: [] for e in self.ENG}
        self.cnt, self.sem = {}, {}
        self.seen = {e: {} for e in self.ENG}
        self.res = {}
        self.inherit = {}
        self.live_prefixes = set()
        for e in ("pe", "act", "dve"):
            self.new_sem(e)

    def new_sem(self, name):
        self.sem[name] = self.es.enter_context(self.nc.semaphore(name))
        self.cnt[name] = 0

    def _get(self, k):
        r = self.res.get(k)
        if r is None:
            inh = self.inherit.get(k[0])
            if inh:
                return [None, inh]
        return r

    def _deps(self, reads, writes):
        need = {}

        def add(s, v):
            if need.get(s, 0) < v:
                need[s] = v
        for k in reads:
            r = self._get(k)
            if r and r[0]:
                add(*r[0])
            if r and k[0] == "ps":
                for s, v in r[1].items():
                    add(s, v)
        for k in writes:
            r = self._get(k)
            if r:
                if r[0]:
                    add(*r[0])
                for s, v in r[1].items():
                    add(s, v)
        return need

    def _waits(self, eng, need):
        for s, v in need.items():
            if s == "pe" and eng == "pe":
                continue
            if self.seen[eng].get(s, 0) >= v:
                continue
            self.seen[eng][s] = v
            self.streams[eng].append(("wait", s, v))

    def _mark(self, reads, writes, sv):
        for k in reads:
            r = self.res.get(k)
            if r is None:
                inh = self.inherit.get(k[0])
                r = self.res[k] = [None, dict(inh) if inh else {}]
            if r[1].get(sv[0], 0) < sv[1]:
                r[1][sv[0]] = sv[1]
        for k in writes:
            self.res[k] = [sv, {}]

    def op(self, eng, fn, reads=(), writes=()):
        self._waits(eng, self._deps(reads, writes))
        self.cnt[eng] += 1
        self.streams[eng].append(("op", fn, eng, 1))
        self._mark(reads, writes, (eng, self.cnt[eng]))

    def dma(self, queue, slot, fn, reads=(), writes=(), inc=16):
        fns = fn if isinstance(fn, (list, tuple)) else [fn]
        if slot not in self.sem:
            self.new_sem(slot)
        need = self._deps(reads, writes)
        if self.cnt[slot] > 0:
            need[slot] = max(need.get(slot, 0), self.cnt[slot])
        self._waits(queue, need)
        for f in fns:
            self.cnt[slot] += inc
            self.streams[queue].append(("op", f, slot, inc))
        self._mark(reads, writes, (slot, self.cnt[slot]))

    def wait_all(self, queue, keys):
        self._waits(queue, self._deps(keys, ()))

    def phase(self, prefixes):
        need = {}
        for k in list(self.res.keys()):
            if k[0] in self.live_prefixes:
                r = self.res.pop(k)
                if r[0] and need.get(r[0][0], 0) < r[0][1]:
                    need[r[0][0]] = r[0][1]
                for s, v in r[1].items():
                    if need.get(s, 0) < v:
                        need[s] = v
        for p in self.live_prefixes:
            for s, v in self.inherit.get(p, {}).items():
                if need.get(s, 0) < v:
                    need[s] = v
            self.inherit.pop(p, None)
        self.live_prefixes = set(prefixes)
        for p in prefixes:
            self.inherit[p] = dict(need)

    def replay(self, block):
        def mk(name):
            def body(e):
                for it in self.streams[name]:
                    if it[0] == "wait":
                        e.wait_ge(self.sem[it[1]], it[2])
                    else:
                        it[1](e).then_inc(self.sem[it[2]], it[3])
            return body
        block.tensor(mk("pe"))
        block.scalar(mk("act"))
        block.vector(mk("dve"))
        block.gpsimd(mk("pool"))
        block.sync(mk("sp"))


class Prog:
    def __init__(self, layers, final_norm, load_h_name="xT"):
        self.layers = layers
        self.final_norm = final_norm
        self.nc = nc = bass.Bass("TRN2", target_bir_lowering=False)
        self.dram = {}
        self.es = ExitStack()
        self.in_names = []
        self.load_h_name = load_h_name

    def din(self, name, shape, dt=F32):
        if name not in self.dram:
            self.dram[name] = self.nc.dram_tensor(name, list(shape), dt, kind="ExternalInput").ap()
            self.in_names.append(name)
        return self.dram[name]

    def sb(self, name, shape, dt):
        return self.es.enter_context(self.nc.sbuf_tensor(name, list(shape), dt))

    def build(self):
        nc, es = self.nc, self.es
        with es:
            self.out = nc.dram_tensor("outT", [D, T], F32, kind="ExternalOutput").ap()
            self.H = self.sb("H", [128, KC, T], F32)
            self.HN = self.sb("HN", [128, KC, T], BF16)
            self.S = self.sb("S", [128, S_BYTES // 2], BF16)
            self.RING = self.sb("RING", [128, 3, 4096], BF16)
            self.PT = self.sb("PT", [128, 2, 2, TT], BF16)
            self.WPJ = self.sb("WPJ", [128, 2, D], BF16)
            self.SQ = self.sb("SQ", [128, 3, TT], BF16)
            self.RSTD = self.sb("RSTD", [128, 2, TT], F32)
            self.TMP = self.sb("TMP", [128, 3, TT], F32)
            self.TMP2 = self.sb("TMP2", [128, 2, TT], F32)
            self.DIAG = self.sb("DIAG", [128, 40, 128], BF16)
            self.CF = self.sb("CF", [128, NCF], F32)
            self.CM = self.sb("CM", [128, NCM], BF16)
            self.SMALL = self.sb("SMALL", [128, 64], F32)
            self.PS = es.enter_context(nc.psum_tensor("PS", [128, 8, TT], F32))
            self.sch = Sched(nc, es)
            self.ring_i = 0
            self.bank_i = 0
            self.tmp_i = 0
            self.tmp2_i = 0
            self.sq_i = 0
            self.rstd_i = 0
            self.pt_i = 0
            self.xch_i = 0
            self.emit()
            with nc.Block() as block:
                self.sch.replay(block)
        return nc

    def bank(self, pool=(0, 1, 2, 3, 4, 5)):
        b = pool[self.bank_i % len(pool)]
        self.bank_i += 1
        return b

    def ring(self, n_elems, src_ap_fn, view_fn, reads=(), nsplit=0):
        s = self.ring_i % 3
        self.ring_i += 1
        dst_flat = self.RING[:, s, 0:n_elems]
        view = view_fn(dst_flat)
        key = ("ring", s)
        if nsplit:
            fns = [lambda e, v=view, a=src_ap_fn, i=i: e.dma_start(out=v[:, :, i, :], in_=a()[:, :, i, :])
                   for i in range(nsplit)]
        else:
            fns = [lambda e, v=view, a=src_ap_fn: e.dma_start(out=v, in_=a())]
        self.sch.dma("pool", f"ring{s}", fns, reads=reads, writes=[key])
        return view, key

    def mm(self, out, lhsT, rhs, start, stop, reads, writes):
        self.sch.op("pe", lambda e: e.matmul(out, lhsT, rhs, start=start, stop=stop), reads, writes)

    def act(self, out, in_, func, reads, writes, bias=None, scale=None):
        kw = {}
        if bias is not None:
            kw["bias"] = bias
        if scale is not None:
            kw["scale"] = scale
        self.sch.op("act", lambda e: e.activation(out=out, in_=in_, func=func, **kw), reads, writes)

    def dve(self, fn, reads, writes):
        self.sch.op("dve", fn, reads, writes)

    def tsl(self, t):
        return slice(t * TT, (t + 1) * TT)

    def emit(self):
        sch = self.sch
        cf = self.din("cf", [128, NCF])
        cm = self.din("cm", [128, NCM])
        xT = self.din(self.load_h_name, [D, T])
        sch.dma("sp", "ld_cf", lambda e: e.dma_start(out=self.CF[:, :], in_=cf), writes=[("CF",)])
        sch.dma("pool", "ld_cm", lambda e: e.dma_start(out=self.CM[:, :], in_=cm), writes=[("CM",)])
        for c in range(KC):
            sch.dma("sp", f"ld_h{c % 4}",
                    lambda e, c=c: e.dma_start(out=self.H[:, c, :], in_=xT[c * 128:(c + 1) * 128, :]),
                    writes=[("H", c, t) for t in range(NT)])
        self.EPSC = self.SMALL[:, 60:61]
        self.dve(lambda e: e.memset(self.EPSC, EPS), [], [("EPSC",)])
        for li in self.layers:
            kind, j = li % 3, li // 3
            self.rmsnorm(C_NMIX + li * 8)
            if kind == 0:
                self.conv_mixer(j)
            elif kind == 1:
                self.attn_mixer(j)
            else:
                self.lru_mixer(j)
            if MIXER_ONLY:
                continue
            self.rmsnorm(C_NMLP + li * 8)
            self.mlp(li)
            self.rmsnorm(C_NPLE + li * 8)
            self.ple(li)
        if self.final_norm:
            self.rmsnorm(C_NFIN, final=True)
        for c in range(KC):
            sch.dma("sp", f"st{c % 4}",
                    lambda e, c=c: e.dma_start(out=self.out[c * 128:(c + 1) * 128, :], in_=self.H[:, c, :]),
                    reads=[("H", c, t) for t in range(NT)])
        sch.wait_all("sp", [("H", c, t) for c in range(KC) for t in range(NT)])
        for s in range(4):
            name = f"st{s}"
            sch.streams["sp"].append(("wait", name, sch.cnt[name]))

    def rmsnorm(self, gcol, final=False, tiles=None):
        sch = self.sch
        onesD = self.CM[:, M_ONESD:M_ONESD + 128]
        for t in (tiles or range(NT)):
            ts = self.tsl(t)
            b = self.bank((6, 7))
            ps = self.PS[:, b, :]
            for c in range(KC):
                q = self.sq_i % 3
                self.sq_i += 1
                sq = self.SQ[:, q, :]
                self.act(sq, self.H[:, c, ts], AF.Square, [("H", c, t)], [("SQ", q)])
                self.mm(ps, onesD, sq, c == 0, c == KC - 1, [("SQ", q), ("CM",)], [("ps", b)])
            r = self.rstd_i % 2
            self.rstd_i += 1
            rstd = self.RSTD[:, r, :]
            self.act(rstd, ps, AF.Sqrt, [("ps", b), ("EPSC",)], [("RSTD", r)], bias=self.EPSC)
            self.dve(lambda e, rstd=rstd: e.reciprocal(rstd, rstd), [("RSTD", r)], [("RSTD", r)])
            for c in range(KC):
                g = self.CF[:, gcol + c:gcol + c + 1]
                if final:
                    out = self.H[:, c, ts]
                    wk = [("H", c, t)]
                else:
                    out = self.HN[:, c, ts]
                    wk = [("HN", c, t)]
                self.dve(lambda e, out=out, c=c, ts=ts, g=g, rstd=rstd:
                         e.scalar_tensor_tensor(out, self.H[:, c, ts], g, rstd, ALU.mult, ALU.mult),
                         [("H", c, t), ("RSTD", r), ("CF",)], wk)

    def mlp(self, li):
        sch = self.sch
        w_up = self.din("mlp_w_up", [DEPTH, D, 4 * D])
        w_dn = self.din("mlp_w_down", [DEPTH, 4 * D, D])
        sch.phase(["A"])
        A = self.S[:, 0:8 * T].rearrange("p (k t) -> p k t", k=8)
        for qd in range(4):
            for jj in range(2):
                f0 = qd * 1024 + jj * 512
                wv, wk = self.ring(
                    4096,
                    lambda f0=f0: w_up[li, :, f0:f0 + 512].rearrange("(k p) f -> p k f", p=128),
                    lambda d: d.rearrange("p (k f) -> p k f", k=8))
                for cc in range(4):
                    fl = jj * 4 + cc
                    for t in range(NT):
                        ts = self.tsl(t)
                        b = self.bank()
                        ps = self.PS[:, b, :]
                        for kc in range(KC):
                            self.mm(ps, wv[:, kc, cc * 128:(cc + 1) * 128], self.HN[:, kc, ts],
                                    kc == 0, kc == KC - 1, [wk, ("HN", kc, t)], [("ps", b)])
                        q = self.tmp_i % 3
                        self.tmp_i += 1
                        tmp = self.TMP[:, q, :]
                        self.act(tmp, ps, AF.Relu, [("ps", b)], [("TMP", q)])
                        self.dve(lambda e, o=A[:, fl, ts], tmp=tmp: e.tensor_tensor(o, tmp, tmp, ALU.mult),
                                 [("TMP", q)], [("A", fl, t)])
            for j2 in range(4):
                r0 = qd * 1024
                wv, wk = self.ring(
                    2048,
                    lambda r0=r0, j2=j2: w_dn[li, r0:r0 + 1024, j2 * 256:(j2 + 1) * 256].rearrange("(k p) f -> p k f", p=128),
                    lambda d: d.rearrange("p (k f) -> p k f", k=8))
                for mm_ in range(2):
                    m = j2 * 2 + mm_
                    for t in range(NT):
                        ts = self.tsl(t)
                        b = self.bank()
                        ps = self.PS[:, b, :]
                        for k in range(8):
                            self.mm(ps, wv[:, k, mm_ * 128:(mm_ + 1) * 128], A[:, k, ts],
                                    k == 0, k == 7, [wk, ("A", k, t)], [("ps", b)])
                        self.dve(lambda e, h=self.H[:, m, ts], ps=ps: e.tensor_tensor(h, ps, h, ALU.add),
                                 [("ps", b), ("H", m, t)], [("H", m, t)])

    def ple(self, li):
        sch = self.sch
        w_g = self.din("ple_w_gate", [DEPTH, D, D])
        w_p = self.din("ple_w_proj", [DEPTH, 256, D])
        pT = self.din("pT", [DEPTH, 256, T])
        sch.dma("pool", "ld_wpj",
                lambda e: e.dma_start(out=self.WPJ[:, :, :], in_=w_p[li].rearrange("(k p) f -> p k f", p=128)),
                writes=[("WPJ",)])
        for jj in range(2):
            wv, wk = self.ring(
                4096,
                lambda jj=jj: w_g[li, :, jj * 512:(jj + 1) * 512].rearrange("(k p) f -> p k f", p=128),
                lambda d: d.rearrange("p (k f) -> p k f", k=8))
            for t in range(NT):
                ts = self.tsl(t)
                pi = self.pt_i % 2
                self.pt_i += 1
                sch.dma("pool", f"ld_pt{pi}",
                        lambda e, pi=pi, ts=ts: e.dma_start(out=self.PT[:, pi, :, :],
                                                            in_=pT[li, :, ts].rearrange("(k p) t -> p k t", p=128)),
                        writes=[("PT", pi)])
                for cc in range(4):
                    m = jj * 4 + cc
                    b = self.bank()
                    ps = self.PS[:, b, :]
                    for kc in range(KC):
                        self.mm(ps, wv[:, kc, cc * 128:(cc + 1) * 128], self.HN[:, kc, ts],
                                kc == 0, kc == KC - 1, [wk, ("HN", kc, t)], [("ps", b)])
                    b2 = self.bank()
                    ps2 = self.PS[:, b2, :]
                    for k2 in range(2):
                        self.mm(ps2, self.WPJ[:, k2, m * 128:(m + 1) * 128], self.PT[:, pi, k2, :],
                                k2 == 0, k2 == 1, [("WPJ",), ("PT", pi)], [("ps", b2)])
                    q = self.tmp_i % 3
                    self.tmp_i += 1
                    g = self.TMP[:, q, :]
                    self.act(g, ps, AF.Sigmoid, [("ps", b)], [("TMP", q)])
                    q2 = self.tmp2_i % 2
                    self.tmp2_i += 1
                    t2 = self.TMP2[:, q2, :]
                    self.dve(lambda e, t2=t2, ps2=ps2, g=g: e.tensor_tensor(t2, ps2, g, ALU.mult),
                             [("ps", b2), ("TMP", q)], [("TMP2", q2)])
                    self.dve(lambda e, h=self.H[:, m, ts], t2=t2: e.tensor_tensor(h, t2, h, ALU.add),
                             [("TMP2", q2), ("H", m, t)], [("H", m, t)])

    def exchange(self, src_ap, dst_ap, ncols, dt, reads, writes, parts_in=None, parts_out=None):
        sch, nc = self.sch, self.nc
        i = self.xch_i
        self.xch_i += 1
        src = nc.dram_tensor(f"xs{i}", [128, ncols], dt)
        gat = nc.dram_tensor(f"xg{i}", [256, ncols], dt)
        ks, kg = (f"xs{i}",), (f"xg{i}",)
        if parts_in is None:
            parts_in = [(src_ap, lambda d: d)]
        if parts_out is None:
            parts_out = [(dst_ap, lambda d: d[0:128, :])]
        sch.dma("pool", "xa",
                [lambda e, a=a, f=f: e.dma_start(out=f(src.ap()), in_=a) for a, f in parts_in],
                reads=reads, writes=[ks])
        sch.dma("pool", "xc",
                lambda e: e.collective_compute("AllGather", ALU.bypass, replica_groups=PAIRS,
                                               ins=[src.ap().opt()], outs=[gat.ap().opt()]),
                reads=[ks], writes=[kg], inc=1)
        sch.dma("pool", "xb",
                [lambda e, a=a, f=f: e.dma_start(out=a, in_=f(gat.ap())) for a, f in parts_out],
                reads=[kg], writes=writes)

    def conv_mixer(self, j):
        sch = self.sch
        w_in = self.din("sc_w_in", [2, D, 3 * D])
        w_out = self.din("sc_w_out", [2, D, D])
        sch.phase(["Z", "ZT", "ZH"])
        ident = self.CM[:, M_ID:M_ID + 128]
        for k in range(3):
            for c in range(KC):
                col = C_SCW + (j * 3 + k) * 8 + c
                dg = self.DIAG[:, k * 8 + c, :]
                self.dve(lambda e, dg=dg, col=col: e.tensor_scalar(dg, ident, self.CF[:, col:col + 1], None, ALU.mult),
                         [("CM",), ("CF",)], [("DIAG", k * 8 + c)])
        ZW = T + 2
        Z = self.S[:, 0:8 * ZW].rearrange("p (k t) -> p k t", k=8)
        ZT = self.SMALL[:, 0:8].bitcast(BF16)
        ZH = self.SMALL[:, 8:16].bitcast(BF16)
        flag = self.CF[:, C_FLAG:C_FLAG + 1]
        w3 = w_in[j].rearrange("(k p) (s f) -> p k s f", p=128, s=3)
        for c in range(KC):
            wv, wk = self.ring(2048, lambda c=c: w3[:, :, 1:3, c * 128:(c + 1) * 128],
                               lambda d: d.rearrange("p (k s f) -> p k s f", k=8, s=2), nsplit=2)
            for t in range(NT):
                ts = self.tsl(t)
                bs = []
                for s_ in range(2):
                    b = self.bank()
                    bs.append(b)
                    for kc in range(KC):
                        self.mm(self.PS[:, b, :], wv[:, kc, s_, :], self.HN[:, kc, ts],
                                kc == 0, kc == KC - 1, [wk, ("HN", kc, t)], [("ps", b)])
                q = self.tmp_i % 3
                self.tmp_i += 1
                tmp = self.TMP[:, q, :]
                self.act(tmp, self.PS[:, bs[0], :], AF.Copy, [("ps", bs[0])], [("TMP", q)])
                self.dve(lambda e, o=Z[:, c, 2 + t * TT:2 + (t + 1) * TT], x=self.PS[:, bs[1], :], tmp=tmp:
                         e.tensor_tensor(o, x, tmp, ALU.mult),
                         [("ps", bs[1]), ("TMP", q)], [("Z", c, t)])
        self.dve(lambda e: e.tensor_copy(ZT.rearrange("p (c w) -> p c w", c=8), Z[:, :, T:T + 2]),
                 [("Z", c, NT - 1) for c in range(KC)], [("ZT",)])
        self.exchange(ZT, ZH, 16, BF16, [("ZT",)], [("ZH",)])
        self.dve(lambda e: e.tensor_scalar(Z[:, :, 0:2], ZH.rearrange("p (c w) -> p c w", c=8), flag, None, ALU.mult),
                 [("ZH",), ("CF",)], [("Z", c, "h") for c in range(KC)])
        gb = []
        for hh in range(2):
            gb.append(self.ring(4096, lambda hh=hh: w_in[j, :, hh * 512:(hh + 1) * 512].rearrange("(k p) f -> p k f", p=128),
                                lambda d: d.rearrange("p (k f) -> p k f", k=8)))
        for tgroup in ((3, 2, 1), (0,)):
            for c in range(KC):
                wv, wk = gb[c // 4]
                cl = c % 4
                for t in tgroup:
                    ts = self.tsl(t)
                    bg = self.bank()
                    for kc in range(KC):
                        self.mm(self.PS[:, bg, :], wv[:, kc, cl * 128:(cl + 1) * 128], self.HN[:, kc, ts],
                                kc == 0, kc == KC - 1, [wk, ("HN", kc, t)], [("ps", bg)])
                    bc = self.bank()
                    prev = ("Z", c, t - 1) if t > 0 else ("Z", c, "h")
                    for k in range(3):
                        self.mm(self.PS[:, bc, :], self.DIAG[:, k * 8 + c, :], Z[:, c, t * TT + k:t * TT + k + TT],
                                k == 0, k == 2, [("DIAG", k * 8 + c), ("Z", c, t), prev], [("ps", bc)])
                    q = self.tmp_i % 3
                    self.tmp_i += 1
                    tmp = self.TMP[:, q, :]
                    self.act(tmp, self.PS[:, bg, :], AF.Copy, [("ps", bg)], [("TMP", q)])
                    self.dve(lambda e, o=Z[:, c, 2 + t * TT:2 + (t + 1) * TT], x=self.PS[:, bc, :], tmp=tmp:
                             e.tensor_tensor(o, x, tmp, ALU.mult),
                             [("ps", bc), ("TMP", q)], [("Z", c, t)])
        for jj in range(2):
            wv, wk = self.ring(4096, lambda jj=jj: w_out[j, :, jj * 512:(jj + 1) * 512].rearrange("(k p) f -> p k f", p=128),
                               lambda d: d.rearrange("p (k f) -> p k f", k=8))
            for cc in range(4):
                m = jj * 4 + cc
                for t in range(NT):
                    ts = self.tsl(t)
                    b = self.bank()
                    ps = self.PS[:, b, :]
                    for kc in range(KC):
                        self.mm(ps, wv[:, kc, cc * 128:(cc + 1) * 128], Z[:, kc, 2 + t * TT:2 + (t + 1) * TT],
                                kc == 0, kc == KC - 1, [wk, ("Z", kc, t)], [("ps", b)])
                    self.dve(lambda e, h=self.H[:, m, ts], ps=ps: e.tensor_tensor(h, ps, h, ALU.add),
                             [("ps", b), ("H", m, t)], [("H", m, t)])

    def lru_mixer(self, j):
        sch = self.sch
        w_in = self.din("lru_w_in", [1, D, 2560])
        w_a = self.din("lru_w_a", [1, 10, 128, 128])
        w_x = self.din("lru_w_x", [1, 10, 128, 128])
        w_out = self.din("lru_w_out", [1, 1280, D])
        sch.phase(["G", "XR", "XB", "RA", "I", "M"])
        S = self.S
        G = S[:, 0:2048]
        XR = S[:, 2048:4100]
        XB = S[:, 4104:6152]
        RA = S[:, 6152:10248].bitcast(F32)
        II = S[:, 10248:14344].bitcast(F32)
        MM = S[:, 14344:18440].bitcast(F32)
        ident = self.CM[:, M_ID:M_ID + 128]
        flag = self.CF[:, C_FLAG:C_FLAG + 1]
        SM = self.SMALL
        XT = SM[:, 16:36].bitcast(BF16)
        XH = SM[:, 36:56].bitcast(BF16)
        ONEC = SM[:, 61:62]
        HS = SM[:, 62:63]
        H0 = SM[:, 63:64]
        CN = SM[:, 0:10]
        self.dve(lambda e: e.memset(ONEC, 1.0), [], [("ONEC",)])
        for k in range(4):
            for n in range(10):
                col = C_LCW + k * 10 + n
                dg = self.DIAG[:, k * 10 + n, :]
                self.dve(lambda e, dg=dg, col=col: e.tensor_scalar(dg, ident, self.CF[:, col:col + 1], None, ALU.mult),
                         [("CM",), ("CF",)], [("DIAG", k * 10 + n)])
        self.act(CN, self.CF[:, C_LAM:C_LAM + 10], AF.Exp, [("CF",)], [("CN",)], scale=-1.0)
        self.act(CN, CN, AF.Ln, [("CN",), ("ONEC",)], [("CN",)], bias=ONEC)
        self.dve(lambda e: e.tensor_scalar(CN, CN, -8.0, None, ALU.mult), [("CN",)], [("CN",)])
        WA = self.PT[:, :, :, :].rearrange("p a b t -> p (a b t)")[:, 0:1280].rearrange("p (n k) -> p n k", n=10)
        WX = self.WPJ[:, :, :].rearrange("p a f -> p (a f)")[:, 0:1280].rearrange("p (n k) -> p n k", n=10)
        sch.dma("pool", "ld_pt0", lambda e: e.dma_start(out=WA, in_=w_a[0].rearrange("n j k -> j n k")),
                writes=[("PT", 0), ("PT", 1)])
        sch.dma("pool", "ld_wpj", lambda e: e.dma_start(out=WX, in_=w_x[0].rearrange("n j k -> j n k")),
                writes=[("WPJ",)])
        bh = self.bank((6, 7))
        for (c0, cw) in ((1280, 512), (1792, 512), (2304, 256)):
            wv, wk = self.ring(8 * cw, lambda c0=c0, cw=cw: w_in[0, :, c0:c0 + cw].rearrange("(k p) f -> p k f", p=128),
                               lambda d: d.rearrange("p (k f) -> p k f", k=8))
            for cl in range(cw // 128):
                n = (c0 - 1280) // 128 + cl
                for kc in range(KC):
                    self.mm(self.PS[:, bh, n * 4:(n + 1) * 4], wv[:, kc, cl * 128:(cl + 1) * 128], self.HN[:, kc, T - 4:T],
                            kc == 0, kc == KC - 1, [wk, ("HN", kc, NT - 1)], [("ps", bh)])
        self.act(XT, self.PS[:, bh, 0:40], AF.Copy, [("ps", bh)], [("XT",)])
        self.exchange(XT, XH, 40, BF16, [("XT",)], [("XH",)])
        w2 = w_in[0].rearrange("(k p) (s f) -> p k s f", p=128, s=2)
        for n in range(10):
            wv, wk = self.ring(2048, lambda n=n: w2[:, :, :, n * 128:(n + 1) * 128],
                               lambda d: d.rearrange("p (k s f) -> p k s f", k=8, s=2), nsplit=2)
            for t in range(NT):
                ts = self.tsl(t)
                bg = self.bank()
                for kc in range(KC):
                    self.mm(self.PS[:, bg, :], wv[:, kc, 0, :], self.HN[:, kc, ts], kc == 0, kc == KC - 1,
                            [wk, ("HN", kc, t)], [("ps", bg)])
                self.act(G[:, ts], self.PS[:, bg, :], AF.Gelu, [("ps", bg)], [("G", t)])
                bx = self.bank()
                for kc in range(KC):
                    self.mm(self.PS[:, bx, :], wv[:, kc, 1, :], self.HN[:, kc, ts], kc == 0, kc == KC - 1,
                            [wk, ("HN", kc, t)], [("ps", bx)])
                self.act(XR[:, 4 + t * TT:4 + (t + 1) * TT], self.PS[:, bx, :], AF.Copy, [("ps", bx)], [("XR", t)])
            self.dve(lambda e, n=n: e.tensor_scalar(XR[:, 0:4], XH[:, n * 4:(n + 1) * 4], flag, None, ALU.mult),
                     [("XH",), ("CF",)], [("XR", "h")])
            for t in range(NT):
                ts = self.tsl(t)
                bc = self.bank()
                prev = ("XR", t - 1) if t > 0 else ("XR", "h")
                for k in range(4):
                    self.mm(self.PS[:, bc, :], self.DIAG[:, k * 10 + n, :], XR[:, 1 + t * TT + k:1 + t * TT + k + TT],
                            k == 0, k == 3, [("DIAG", k * 10 + n), ("XR", t), prev], [("ps", bc)])
                self.act(XB[:, ts], self.PS[:, bc, :], AF.Identity, [("ps", bc), ("CF",)], [("XB", t)],
                         bias=self.CF[:, C_LCB + n:C_LCB + n + 1])
                ba = self.bank()
                self.mm(self.PS[:, ba, :], WA[:, n, :], XB[:, ts], True, True, [("PT", 0), ("XB", t)], [("ps", ba)])
                self.act(RA[:, ts], self.PS[:, ba, :], AF.Sigmoid, [("ps", ba), ("CF",)], [("RA", t)],
                         bias=self.CF[:, C_LBA + n:C_LBA + n + 1])
                bx2 = self.bank()
                self.mm(self.PS[:, bx2, :], WX[:, n, :], XB[:, ts], True, True, [("WPJ",), ("XB", t)], [("ps", bx2)])
                self.act(II[:, ts], self.PS[:, bx2, :], AF.Sigmoid, [("ps", bx2), ("CF",)], [("I", t)],
                         bias=self.CF[:, C_LBX + n:C_LBX + n + 1])
            allt = lambda p: [(p, t) for t in range(NT)]
            self.act(RA, RA, AF.Exp, allt("RA") + [("CN",)], allt("RA"), scale=CN[:, n:n + 1])
            self.dve(lambda e: e.tensor_tensor(MM, RA, RA, ALU.mult), allt("RA"), [("M",)])
            self.act(MM, MM, AF.Sqrt, [("M",), ("ONEC",)], [("M",)], bias=ONEC, scale=-1.0)
            self.dve(lambda e: e.tensor_tensor(II, II, XB, ALU.mult), allt("I") + allt("XB"), allt("I"))
            self.dve(lambda e: e.tensor_tensor(II, II, MM, ALU.mult), allt("I") + [("M",)], allt("I"))
            self.dve(lambda e: e.tensor_tensor_scan(MM, RA, II, 0.0, ALU.mult, ALU.add),
                     allt("RA") + allt("I") + [("M",)], [("M",)])
            self.dve(lambda e: e.tensor_copy(HS, MM[:, T - 1:T]), [("M",)], [("HS",)])
            self.dve(lambda e: e.tensor_scalar(II, RA, 0.0, None, ALU.mult), allt("RA") + allt("I"), allt("I"))
            self.dve(lambda e: e.tensor_tensor_scan(II, RA, II, 1.0, ALU.mult, ALU.add),
                     allt("RA") + allt("I"), allt("I"))
            self.exchange(HS, H0, 1, F32, [("HS",)], [("H0",)])
            self.dve(lambda e: e.tensor_tensor(XB, II, G, ALU.mult), allt("I") + allt("G") + allt("XB"), allt("XB"))
            self.dve(lambda e: e.tensor_tensor(G, MM, G, ALU.mult), [("M",)] + allt("G"), allt("G"))
            self.dve(lambda e: e.tensor_scalar(H0, H0, flag, None, ALU.mult), [("H0",), ("CF",)], [("H0",)])
            self.dve(lambda e: e.scalar_tensor_tensor(G, XB, H0, G, ALU.mult, ALU.add),
                     allt("XB") + allt("G") + [("H0",)], allt("G"))
            wo, wok = self.ring(1024, lambda n=n: w_out[0, n * 128:(n + 1) * 128, :], lambda d: d)
            for m in range(KC):
                for t in range(NT):
                    ts = self.tsl(t)
                    b = self.bank()
                    self.mm(self.PS[:, b, :], wo[:, m * 128:(m + 1) * 128], G[:, ts], True, True,
                            [wok, ("G", t)], [("ps", b)])
                    self.dve(lambda e, h=self.H[:, m, ts], ps=self.PS[:, b, :]: e.tensor_tensor(h, ps, h, ALU.add),
                             [("ps", b), ("H", m, t)], [("H", m, t)])

    def rope_tables(self):
        S = self.S
        pos = self.din("pos", [32, T], I32)
        PI = S[0:32, 0:4096].bitcast(I32)
        ANG = S[0:32, 4096:8192].bitcast(F32)
        KF = S[0:32, 8192:12288].bitcast(F32)
        ANC = S[0:32, 12288:16384].bitcast(F32)
        COS = self.TMP2[0:32, :, :].rearrange("p a t -> p (a t)").bitcast(BF16)
        SIN = self.RSTD[0:32, :, :].rearrange("p a t -> p (a t)").bitcast(BF16)
        tk = [("TMP2", 0), ("TMP2", 1)]
        rk = [("RSTD", 0), ("RSTD", 1)]
        self.sch.dma("sp", "ld_pos", lambda e: e.dma_start(out=PI, in_=pos), writes=[("RT",)])
        self.dve(lambda e: e.tensor_copy(ANG, PI), [("RT",)], [("RT",)])
        self.dve(lambda e: e.tensor_scalar(ANG, ANG, self.CF[0:32, C_INVF:C_INVF + 1], None, ALU.mult),
                 [("RT",), ("CF",)], [("RT",)])
        MAGIC = 12582912.0
        TWO_PI = 6.283185307179586
        PI_ = 3.1415925
        for which, dst, dk in ((0, SIN, rk), (1, COS, tk)):
            src = ANG
            if which == 1:
                self.dve(lambda e: e.tensor_scalar(ANC, ANG, 1.5707963267948966, None, ALU.add), [("RT",)], [("RT",)])
                src = ANC
            self.dve(lambda e, src=src: e.tensor_scalar(KF, src, 1.0 / TWO_PI, MAGIC, ALU.mult, ALU.add),
                     [("RT",), ("RT",)], [("RT",)])
            self.dve(lambda e: e.tensor_scalar(KF, KF, -MAGIC, -TWO_PI, ALU.add, ALU.mult), [("RT",)], [("RT",)])
            self.dve(lambda e, src=src: e.tensor_tensor(KF, KF, src, ALU.add), [("RT",), ("RT",), ("RT",)], [("RT",)])
            self.dve(lambda e: e.tensor_scalar(KF, KF, -PI_, PI_, ALU.max, ALU.min), [("RT",)], [("RT",)])
            self.act(dst, KF, AF.Sin, [("RT",)], dk)
        return COS, SIN, tk, rk

    def attn_mixer(self, j):
        sch = self.sch
        w_qkv = self.din("attn_w_qkv", [1, D, 9216])
        w_o = self.din("attn_w_o", [1, D, D])
        sch.phase(["RT"])
        S = self.S
        COS, SIN, tk, rk = self.rope_tables()
        sch.phase(["KT", "V", "QT", "VT", "ND"])
        KT = S[:, 0:6144].rearrange("p (g t) -> p g t", g=3)
        V = S[:, 6144:12288].rearrange("p (b d) -> p b d", d=128)
        QT = S[:, 12288:14336]
        VT = S[:, 14336:16384]
        ND = S[:, 16384:24576].bitcast(F32).rearrange("p (a t) -> p a t", a=2)
        HK = self.DIAG[:, 0:21, :]
        HVa = self.WPJ[:, :, :].rearrange("p a (b d) -> p (a b) d", d=128)
        HVb = self.PT[:, :, :, :].rearrange("p a b t -> p (a b t)")[:, 0:640].rearrange("p (b d) -> p b d", d=128)
        ident = self.CM[:, M_ID:M_ID + 128]
        ones = self.CM[:, M_ONES:M_ONES + 128]
        rot = self.CM[:, M_ROT128:M_ROT128 + 128]
        DIL = (1, 4, 16)
        SCALE = 128.0 ** -0.5
        wq4 = w_qkv[0].rearrange("(k p) (s g h f) -> p k s g h f", p=128, s=3, g=3, h=8)

        def hv(i):
            return (HVa[:, i, :], ("WPJ",)) if i < 16 else (HVb[:, i - 16, :], ("PT", 0))

        def perm_out(dst2d, t, d):
            v = dst2d.rearrange("p (r m) -> p m r", r=d)
            return v[:, t * (TT // d):(t + 1) * (TT // d), :]

        def project(wslice, wk, dst2d, dkey, d, rope):
            for t in range(NT):
                ts = self.tsl(t)
                b = self.bank()
                ps = self.PS[:, b, :]
                for kc in range(KC):
                    self.mm(ps, wslice[:, kc, :], self.HN[:, kc, ts], kc == 0, kc == KC - 1,
                            [wk, ("HN", kc, t)], [("ps", b)])
                pin = ps.rearrange("p (m r) -> p m r", r=d)
                self.act(perm_out(dst2d, t, d), pin, AF.Copy, [("ps", b)], [dkey])
                if rope:
                    q1 = self.tmp_i % 3; self.tmp_i += 1
                    qb = self.TMP[:, q1, :].bitcast(BF16)[:, 0:TT]
                    self.act(qb, ps, AF.Copy, [("ps", b)], [("TMP", q1)])
                    b2 = self.bank()
                    self.mm(self.PS[:, b2, :], rot, qb, True, True, [("TMP", q1), ("CM",)], [("ps", b2)])
                    ps2 = self.PS[0:32, b2, :]
                    q2 = self.tmp_i % 3; self.tmp_i += 1
                    t1 = self.TMP[0:32, q2, :]
                    self.dve(lambda e, t1=t1, ps=ps, ts=ts: e.tensor_tensor(t1, ps[0:32, :], COS[:, ts], ALU.mult),
                             [("ps", b)] + tk, [("TMP", q2)])
                    q3 = self.tmp_i % 3; self.tmp_i += 1
                    t2 = self.TMP[0:32, q3, :]
                    self.dve(lambda e, t2=t2, ps2=ps2, ts=ts: e.tensor_tensor(t2, ps2, SIN[:, ts], ALU.mult),
                             [("ps", b2)] + rk, [("TMP", q3)])
                    self.dve(lambda e, o=perm_out(dst2d[0:32, :], t, d), t1=t1, t2=t2:
                             e.tensor_tensor(o, t1.rearrange("p (m r) -> p m r", r=d), t2.rearrange("p (m r) -> p m r", r=d), ALU.add),
                             [("TMP", q2), ("TMP", q3), dkey], [dkey])

        if ATT_STAGE == 0:
            return
        for h in range(8 if ATT_STAGE == 9 else 1):
            for g in range(3):
                d = DIL[g]
                wv, wk = self.ring(2048, lambda g=g, h=h: wq4[:, :, 1:3, g, h, :],
                                   lambda dd: dd.rearrange("p (k s f) -> p k s f", k=8, s=2), nsplit=2)
                project(wv[:, :, 0, :], wk, KT[:, g, :], ("KT", g), d, ATT_STAGE >= 0.7)
                project(wv[:, :, 1, :], wk, VT, ("VT",), d, False)
                for bq in range(4 if ATT_STAGE >= 1 else 0):
                    b = self.bank()
                    psb = self.PS[:, b, 0:256].bitcast(BF16)
                    for i in range(4):
                        blk = bq * 4 + i
                        self.sch.op("pe", lambda e, o=psb[:, i * 128:(i + 1) * 128], a=VT[:, blk * 128:(blk + 1) * 128]:
                                    e.transpose(o, a, ident), [("VT",), ("CM",)], [("ps", b)])
                    self.act(V[:, g * 16 + bq * 4:g * 16 + bq * 4 + 4, :],
                             psb.rearrange("p (i d) -> p i d", d=128), AF.Copy, [("ps", b)], [("V", g)])
            if ATT_STAGE < 2:
                continue
            kt1 = KT[:, 1, :].rearrange("p (r b c) -> p r b c", r=4, b=4)[:, :, 3, :]
            v1 = V[:, 16:32, :].rearrange("p (r b) d -> p r b d", r=4)[:, :, 3, :]
            NB = 128
            parts_in = [
                (KT[:, 0, 15 * NB:16 * NB], lambda dd: dd[:, 0:NB]),
                (kt1, lambda dd: dd[:, NB:5 * NB].rearrange("p (r c) -> p r c", r=4)),
                (KT[:, 2, :], lambda dd: dd[:, 5 * NB:21 * NB]),
                (V[:, 15, :], lambda dd: dd[:, 21 * NB:22 * NB]),
                (v1, lambda dd: dd[:, 22 * NB:26 * NB].rearrange("p (r c) -> p r c", r=4)),
                (V[:, 32:48, :], lambda dd: dd[:, 26 * NB:42 * NB].rearrange("p (b c) -> p b c", b=16)),
            ]
            parts_out = [
                (HK, lambda gg: gg[0:128, 0:21 * NB].rearrange("p (b c) -> p b c", b=21)),
                (HVa, lambda gg: gg[0:128, 21 * NB:37 * NB].rearrange("p (b c) -> p b c", b=16)),
                (HVb, lambda gg: gg[0:128, 37 * NB:42 * NB].rearrange("p (b c) -> p b c", b=5)),
            ]
            self.exchange(None, None, 42 * NB, BF16,
                          [("KT", 0), ("KT", 1), ("KT", 2), ("V", 0), ("V", 1), ("V", 2)],
                          [("DIAG", i) for i in range(21)] + [("WPJ",), ("PT", 0), ("PT", 1)],
                          parts_in=parts_in, parts_out=parts_out)
            if ATT_STAGE < 3:
                continue
            for g in range(3):
                d = DIL[g]
                nb = 16 // d
                wv, wk = self.ring(1024, lambda g=g, h=h: wq4[:, :, 0, g, h, :],
                                   lambda dd: dd.rearrange("p (k f) -> p k f", k=8))
                project(wv, wk, QT, ("QT",), d, True)
                for jb in range(16):
                    r, jl = jb // nb, jb % nb
                    first = jl == 0
                    qs = QT[:, jb * 128:(jb + 1) * 128]
                    if first:
                        hb = (0, 1 + r, 5 + r)[g]
                        kprev, kpk = HK[:, hb, :], ("DIAG", hb)
                        vprev, vpk = hv(hb)
                        mask = self.CM[:, M_MASKH:M_MASKH + 256]
                    else:
                        kprev, kpk = KT[:, g, (jb - 1) * 128:jb * 128], ("KT", g)
                        vprev, vpk = V[:, g * 16 + jb - 1, :], ("V", g)
                        mask = self.CM[:, M_MASK:M_MASK + 256]
                    b = self.bank()
                    ps = self.PS[:, b, 0:256]
                    self.mm(ps[:, 0:128], kprev, qs, True, True, [kpk, ("QT",)], [("ps", b)])
                    self.mm(ps[:, 128:256], KT[:, g, jb * 128:(jb + 1) * 128], qs, True, True,
                            [("KT", g), ("QT",)], [("ps", b)])
                    q = self.sq_i % 3; self.sq_i += 1
                    P = self.SQ[:, q, 0:256]
                    self.act(P, ps, AF.Exp, [("ps", b)], [("SQ", q)], scale=SCALE)
                    self.dve(lambda e, P=P, mask=mask: e.tensor_tensor(P, P, mask, ALU.mult),
                             [("SQ", q), ("CM",)], [("SQ", q)])
                    b2 = self.bank()
                    po = self.PS[:, b2, 0:256]
                    self.mm(po[:, 0:128], vprev, P[:, 0:128], True, False, [vpk, ("SQ", q)], [("ps", b2)])
                    self.mm(po[:, 0:128], V[:, g * 16 + jb, :], P[:, 128:256], False, True, [("V", g), ("SQ", q)], [("ps", b2)])
                    self.mm(po[:, 128:256], ones, P[:, 0:128], True, False, [("CM",), ("SQ", q)], [("ps", b2)])
                    self.mm(po[:, 128:256], ones, P[:, 128:256], False, True, [("CM",), ("SQ", q)], [("ps", b2)])
                    st = jl * 128 * d + r
                    dst = ND[:, :, st:st + 127 * d + 1:d]
                    pin = po.rearrange("p (a c) -> p a c", a=2)
                    if g == 0:
                        self.dve(lambda e, dst=dst, pin=pin: e.tensor_copy(dst, pin), [("ps", b2)], [("ND",)])
                    else:
                        self.dve(lambda e, dst=dst, pin=pin: e.tensor_tensor(dst, pin, dst, ALU.add),
                                 [("ps", b2), ("ND",)], [("ND",)])
            self.dve(lambda e: e.reciprocal(ND[:, 1, :], ND[:, 1, :]), [("ND",)], [("ND",)])
            self.dve(lambda e: e.tensor_tensor(VT, ND[:, 0, :], ND[:, 1, :], ALU.mult), [("ND",), ("VT",)], [("VT",)])
            wo, wok = self.ring(1024, lambda h=h: w_o[0, h * 128:(h + 1) * 128, :], lambda dd: dd)
            for m in range(KC):
                for t in range(NT):
                    ts = self.tsl(t)
                    b = self.bank()
                    self.mm(self.PS[:, b, :], wo[:, m * 128:(m + 1) * 128], VT[:, ts], True, True,
                            [wok, ("VT",)], [("ps", b)])
                    self.dve(lambda e, hh=self.H[:, m, ts], ps=self.PS[:, b, :]: e.tensor_tensor(hh, ps, hh, ALU.add),
                             [("ps", b), ("H", m, t)], [("H", m, t)])


_CACHE = {}


def _const_tables(flag):
    cm = np.zeros((128, NCM), np.float32)
    cm[:, M_ID:M_ID + 128] = np.eye(128, dtype=np.float32)
    cm[:, M_ONESD:M_ONESD + 128] = 1.0 / 1024.0
    cm[:, M_ONES:M_ONES + 128] = 1.0
    for m in range(16):
        cm[m + 16, M_ROT + m] = -1.0
        cm[m, M_ROT + 16 + m] = 1.0
        cm[m + 16, M_ROT128 + m] = -1.0
        cm[m, M_ROT128 + 16 + m] = 1.0
    k = np.arange(128)[:, None]
    q = np.arange(128)[None, :]
    maskP = (k >= q).astype(np.float32)
    maskO = (k <= q).astype(np.float32)
    cm[:, M_MASK:M_MASK + 128] = maskP
    cm[:, M_MASK + 128:M_MASK + 256] = maskO
    cm[:, M_MASKH:M_MASKH + 128] = maskP * flag
    cm[:, M_MASKH + 128:M_MASKH + 256] = maskO
    return cm


def _cols(v):
    v = np.asarray(v, np.float32)
    return np.ascontiguousarray(v.reshape(-1, 128).T)


def _cf_table(inp, flag):
    cf = np.zeros((128, NCF), np.float32)
    for l in range(DEPTH):
        cf[:, C_NMIX + l * 8:C_NMIX + l * 8 + 8] = _cols(inp["norm_mix"][l])
        cf[:, C_NMLP + l * 8:C_NMLP + l * 8 + 8] = _cols(inp["norm_mlp"][l])
        cf[:, C_NPLE + l * 8:C_NPLE + l * 8 + 8] = _cols(inp["norm_ple"][l])
    cf[:, C_NFIN:C_NFIN + 8] = _cols(inp["norm_final"])
    for jj in range(2):
        for k in range(3):
            c0 = C_SCW + (jj * 3 + k) * 8
            cf[:, c0:c0 + 8] = _cols(inp["sc_w_conv"][jj, k])
    for k in range(4):
        cf[:, C_LCW + k * 10:C_LCW + k * 10 + 10] = _cols(inp["lru_conv_w"][0, k])
    cf[:, C_LCB:C_LCB + 10] = _cols(inp["lru_conv_b"][0])
    cf[:, C_LBA:C_LBA + 10] = _cols(inp["lru_b_a"][0])
    cf[:, C_LBX:C_LBX + 10] = _cols(inp["lru_b_x"][0])
    cf[:, C_LAM:C_LAM + 10] = _cols(inp["lru_lambda"][0])
    cf[:, C_FLAG] = flag
    half = 16
    inv = (500000.0 ** (-2.0 * np.arange(half, dtype=np.float32) / 32.0)).astype(np.float32)
    cf[0:16, C_INVF] = inv
    cf[16:32, C_INVF] = inv
    return cf


WEIGHT_NAMES = ["sc_w_in", "sc_w_out", "attn_w_qkv", "attn_w_o", "lru_w_in", "lru_w_a", "lru_w_x",
                "lru_w_out", "mlp_w_up", "mlp_w_down", "ple_w_gate", "ple_w_proj"]


def run_layers(inp, hT_per_core, layers, final_norm):
    key = (tuple(layers), final_norm)
    if key not in _CACHE:
        prog = Prog(list(layers), final_norm)
        nc = prog.build()
        _CACHE[key] = (prog, nc)
    prog, nc = _CACHE[key]
    in_maps = []
    for c in range(N_CORES):
        b, half = c // 2, c % 2
        tok = slice(half * T, (half + 1) * T)
        m = {}
        for name in prog.in_names:
            if name == "cf":
                m[name] = _cf_table(inp, float(half))
            elif name == "cm":
                m[name] = _const_tables(float(half))
            elif name == "xT":
                m[name] = np.ascontiguousarray(hT_per_core[c], dtype=np.float32)
            elif name == "pT":
                m[name] = np.ascontiguousarray(np.transpose(inp["p"][:, b, tok, :], (0, 2, 1)), dtype=np.float32)
            elif name == "pos":
                m[name] = np.ascontiguousarray(np.broadcast_to(inp["positions"][b, tok].reshape(1, T), (32, T)), dtype=np.int32)
            else:
                m[name] = np.ascontiguousarray(inp[name], dtype=np.float32)
        in_maps.append(m)
    res = run_bass_kernel_spmd(nc, in_maps, core_ids=list(range(N_CORES)))
    return [res.results[c]["outT"] for c in range(N_CORES)]


LAUNCH_PLAN = [([0, 1, 2, 3], True)]


def kernel(**inputs):
    inp = {k: np.asarray(v) for k, v in inputs.items()}
    x = inp["x"]
    hT = []
    for c in range(N_CORES):
        b, half = c // 2, c % 2
        hT.append(np.ascontiguousarray(x[b, half * T:(half + 1) * T, :].T))
    for layers, fin in LAUNCH_PLAN:
        hT = run_layers(inp, hT, layers, fin)
    out = np.empty((4, 2 * T, D), np.float32)
    for c in range(N_CORES):
        b, half = c // 2, c % 2
        out[b, half * T:(half + 1) * T, :] = hT[c].T
    return out
```

```python
import numpy as np
from contextlib import ExitStack
import concourse.bass as bass
import concourse.mybir as mybir
from concourse.bass_utils import run_bass_kernel_spmd

F32, BF16, I32 = mybir.dt.float32, mybir.dt.bfloat16, mybir.dt.int32
AF = mybir.ActivationFunctionType
ALU = mybir.AluOpType

T = 2048
TT = 512
NT = T // TT
D = 1024
KC = 8
DEPTH = 4
EPS = 1e-6
N_CORES = 8
PAIRS = [[0, 1], [2, 3], [4, 5], [6, 7]]

C_NMIX = 0
C_NMLP = 32
C_NPLE = 64
C_NFIN = 96
C_SCW = 104
C_LCW = 152
C_LCB = 192
C_LBA = 202
C_LBX = 212
C_LAM = 222
C_FLAG = 232
C_INVF = 233
NCF = 234
M_ID = 0
M_ONESD = 128
M_ONES = 256
M_ROT = 384
M_MASK = 416
M_MASKH = 672
M_ROT128 = 928
NCM = 1056

S_BYTES = 50176
MIXER_ONLY = False
BANKS8 = False
LRU_PIPE = False
ATT_STAGE = 9


class Sched:
    ENG = ("pe", "act", "dve", "pool", "sp")

    def __init__(self, nc, es):
        self.nc, self.es = nc, es
        self.streams = {e: [] for e in self.ENG}
        self.cnt, self.sem = {}, {}
        self.seen = {e: {} for e in self.ENG}
        self.res = {}
        self.inherit = {}
        self.live_prefixes = set()
        for e in ("pe", "act", "dve"):
            self.new_sem(e)

    def new_sem(self, name):
        self.sem[name] = self.es.enter_context(self.nc.semaphore(name))
        self.cnt[name] = 0

    def _get(self, k):
        r = self.res.get(k)
        if r is None:
            inh = self.inherit.get(k[0])
            if inh:
                return [None, inh]
        return r

    def _deps(self, reads, writes):
        need = {}

        def add(s, v):
            if need.get(s, 0) < v:
                need[s] = v
        for k in reads:
            r = self._get(k)
            if r and r[0]:
                add(*r[0])
            if r and k[0] == "ps":
                for s, v in r[1].items():
                    add(s, v)
        for k in writes:
            r = self._get(k)
            if r:
                if r[0]:
                    add(*r[0])
                for s, v in r[1].items():
                    add(s, v)
        return need

    def _waits(self, eng, need):
        for s, v in need.items():
            if s == "pe" and eng == "pe":
                continue
            if self.seen[eng].get(s, 0) >= v:
                continue
            self.seen[eng][s] = v
            self.streams[eng].append(("wait", s, v))

    def _mark(self, reads, writes, sv):
        for k in reads:
            r = self.res.get(k)
            if r is None:
                inh = self.inherit.get(k[0])
                r = self.res[k] = [None, dict(inh) if inh else {}]
            if r[1].get(sv[0], 0) < sv[1]:
                r[1][sv[0]] = sv[1]
        for k in writes:
            self.res[k] = [sv, {}]

    def op(self, eng, fn, reads=(), writes=()):
        self._waits(eng, self._deps(reads, writes))
        self.cnt[eng] += 1
        self.streams[eng].append(("op", fn, eng, 1))
        self._mark(reads, writes, (eng, self.cnt[eng]))

    def dma(self, queue, slot, fn, reads=(), writes=(), inc=16):
        fns = fn if isinstance(fn, (list, tuple)) else [fn]
        if slot not in self.sem:
            self.new_sem(slot)
        need = self._deps(reads, writes)
        if self.cnt[slot] > 0:
            need[slot] = max(need.get(slot, 0), self.cnt[slot])
        self._waits(queue, need)
        for f in fns:
            self.cnt[slot] += inc
            self.streams[queue].append(("op", f, slot, inc))
        self._mark(reads, writes, (slot, self.cnt[slot]))

    def wait_all(self, queue, keys):
        self._waits(queue, self._deps(keys, ()))

    def phase(self, prefixes):
        need = {}
        for k in list(self.res.keys()):
            if k[0] in self.live_prefixes:
                r = self.res.pop(k)
                if r[0] and need.get(r[0][0], 0) < r[0][1]:
                    need[r[0][0]] = r[0][1]
                for s, v in r[1].items():
                    if need.get(s, 0) < v:
                        need[s] = v
        for p in self.live_prefixes:
            for s, v in self.inherit.get(p, {}).items():
                if need.get(s, 0) < v:
                    need[s] = v
            self.inherit.pop(p, None)
        self.live_prefixes = set(prefixes)
        for p in prefixes:
            self.inherit[p] = dict(need)

    def replay(self, block):
        def mk(name):
            def body(e):
                for it in self.streams[name]:
                    if it[0] == "wait":
                        e.wait_ge(self.sem[it[1]], it[2])
                    else:
                        it[1](e).then_inc(self.sem[it[2]], it[3])
            return body
        block.tensor(mk("pe"))
        block.scalar(mk("act"))
        block.vector(mk("dve"))
        block.gpsimd(mk("pool"))
        block.sync(mk("sp"))


class Prog:
    def __init__(self, layers, final_norm, load_h_name="xT"):
        self.layers = layers
        self.final_norm = final_norm
        self.nc = nc = bass.Bass("TRN2", target_bir_lowering=False)
        self.dram = {}
        self.es = ExitStack()
        self.in_names = []
        self.load_h_name = load_h_name

    def din(self, name, shape, dt=F32):
        if name not in self.dram:
            self.dram[name] = self.nc.dram_tensor(name, list(shape), dt, kind="ExternalInput").ap()
            self.in_names.append(name)
        return self.dram[name]

    def sb(self, name, shape, dt):
        return self.es.enter_context(self.nc.sbuf_tensor(name, list(shape), dt))

    def build(self):
        nc, es = self.nc, self.es
        with es:
            self.out = nc.dram_tensor("outT", [D, T], F32, kind="ExternalOutput").ap()
            self.H = self.sb("H", [128, KC, T], F32)
            self.HN = self.sb("HN", [128, KC, T], BF16)
            self.S = self.sb("S", [128, S_BYTES // 2], BF16)
            self.RING = self.sb("RING", [128, 3, 4096], BF16)
            self.PT = self.sb("PT", [128, 2, 2, TT], BF16)
            self.WPJ = self.sb("WPJ", [128, 2, D], BF16)
            self.SQ = self.sb("SQ", [128, 3, TT], BF16)
            self.RSTD = self.sb("RSTD", [128, 2, TT], F32)
            self.TMP = self.sb("TMP", [128, 3, TT], F32)
            self.TMP2 = self.sb("TMP2", [128, 2, TT], F32)
            self.DIAG = self.sb("DIAG", [128, 40, 128], BF16)
            self.CF = self.sb("CF", [128, NCF], F32)
            self.CM = self.sb("CM", [128, NCM], BF16)
            self.SMALL = self.sb("SMALL", [128, 64], F32)
            self.PS = es.enter_context(nc.psum_tensor("PS", [128, 8, TT], F32))
            self.sch = Sched(nc, es)
            self.ring_i = 0
            self.bank_i = 0
            self.tmp_i = 0
            self.tmp2_i = 0
            self.sq_i = 0
            self.rstd_i = 0
            self.pt_i = 0
            self.xch_i = 0
            self.rt_i = 0
            self.p_i = 0
            self.emit()
            with nc.Block() as block:
                self.sch.replay(block)
        return nc

    def bank(self, pool=None, aux=False):
        if pool is None:
            pool = (0, 1, 2, 3, 4, 5, 6, 7) if BANKS8 else ((6, 7) if aux else (0, 1, 2, 3, 4, 5))
        b = pool[self.bank_i % len(pool)]
        self.bank_i += 1
        return b

    def ring(self, n_elems, src_ap_fn, view_fn, reads=(), nsplit=0):
        s = self.ring_i % 3
        self.ring_i += 1
        dst_flat = self.RING[:, s, 0:n_elems]
        view = view_fn(dst_flat)
        key = ("ring", s)
        if nsplit:
            fns = [lambda e, v=view, a=src_ap_fn, i=i: e.dma_start(out=v[:, :, i, :], in_=a()[:, :, i, :])
                   for i in range(nsplit)]
        else:
            fns = [lambda e, v=view, a=src_ap_fn: e.dma_start(out=v, in_=a())]
        self.sch.dma("pool", f"ring{s}", fns, reads=reads, writes=[key])
        return view, key

    def mm(self, out, lhsT, rhs, start, stop, reads, writes):
        self.sch.op("pe", lambda e: e.matmul(out, lhsT, rhs, start=start, stop=stop), reads, writes)

    def act(self, out, in_, func, reads, writes, bias=None, scale=None):
        kw = {}
        if bias is not None:
            kw["bias"] = bias
        if scale is not None:
            kw["scale"] = scale
        self.sch.op("act", lambda e: e.activation(out=out, in_=in_, func=func, **kw), reads, writes)

    def dve(self, fn, reads, writes):
        self.sch.op("dve", fn, reads, writes)

    def tsl(self, t):
        return slice(t * TT, (t + 1) * TT)

    def emit(self):
        sch = self.sch
        cf = self.din("cf", [128, NCF])
        cm = self.din("cm", [128, NCM])
        xT = self.din(self.load_h_name, [D, T])
        sch.dma("sp", "ld_cf", lambda e: e.dma_start(out=self.CF[:, :], in_=cf), writes=[("CF",)])
        sch.dma("pool", "ld_cm", lambda e: e.dma_start(out=self.CM[:, :], in_=cm), writes=[("CM",)])
        for c in range(KC):
            sch.dma("sp", f"ld_h{c % 4}",
                    lambda e, c=c: e.dma_start(out=self.H[:, c, :], in_=xT[c * 128:(c + 1) * 128, :]),
                    writes=[("H", c, t) for t in range(NT)])
        self.EPSC = self.SMALL[:, 60:61]
        self.dve(lambda e: e.memset(self.EPSC, EPS), [], [("EPSC",)])
        for li in self.layers:
            kind, j = li % 3, li // 3
            self.rmsnorm(C_NMIX + li * 8)
            if kind == 0:
                self.conv_mixer(j)
            elif kind == 1:
                self.attn_mixer(j)
            else:
                self.lru_mixer(j)
            if MIXER_ONLY:
                continue
            self.rmsnorm(C_NMLP + li * 8)
            self.mlp(li)
            self.rmsnorm(C_NPLE + li * 8)
            self.ple(li)
        if self.final_norm:
            self.rmsnorm(C_NFIN, final=True)
        for c in range(KC):
            sch.dma("sp", f"st{c % 4}",
                    lambda e, c=c: e.dma_start(out=self.out[c * 128:(c + 1) * 128, :], in_=self.H[:, c, :]),
                    reads=[("H", c, t) for t in range(NT)])
        sch.wait_all("sp", [("H", c, t) for c in range(KC) for t in range(NT)])
        for s in range(4):
            name = f"st{s}"
            sch.streams["sp"].append(("wait", name, sch.cnt[name]))

    def rmsnorm(self, gcol, final=False, tiles=None):
        sch = self.sch
        onesD = self.CM[:, M_ONESD:M_ONESD + 128]
        for t in (tiles or range(NT)):
            ts = self.tsl(t)
            b = self.bank(aux=True)
            ps = self.PS[:, b, :]
            for c in range(KC):
                q = self.sq_i % 3
                self.sq_i += 1
                sq = self.SQ[:, q, :]
                self.act(sq, self.H[:, c, ts], AF.Square, [("H", c, t)], [("SQ", q)])
                self.mm(ps, onesD, sq, c == 0, c == KC - 1, [("SQ", q), ("CM",)], [("ps", b)])
            r = self.rstd_i % 2
            self.rstd_i += 1
            rstd = self.RSTD[:, r, :]
            self.act(rstd, ps, AF.Sqrt, [("ps", b), ("EPSC",)], [("RSTD", r)], bias=self.EPSC)
            self.dve(lambda e, rstd=rstd: e.reciprocal(rstd, rstd), [("RSTD", r)], [("RSTD", r)])
            for c in range(KC):
                g = self.CF[:, gcol + c:gcol + c + 1]
                if final:
                    out = self.H[:, c, ts]
                    wk = [("H", c, t)]
                else:
                    out = self.HN[:, c, ts]
                    wk = [("HN", c, t)]
                self.dve(lambda e, out=out, c=c, ts=ts, g=g, rstd=rstd:
                         e.scalar_tensor_tensor(out, self.H[:, c, ts], g, rstd, ALU.mult, ALU.mult),
                         [("H", c, t), ("RSTD", r), ("CF",)], wk)

    def mlp(self, li):
        sch = self.sch
        w_up = self.din("mlp_w_up", [DEPTH, D, 4 * D])
        w_dn = self.din("mlp_w_down", [DEPTH, 4 * D, D])
        sch.phase(["A"])
        A = self.S[:, 0:8 * T].rearrange("p (k t) -> p k t", k=8)
        for qd in range(4):
            for jj in range(2):
                f0 = qd * 1024 + jj * 512
                wv, wk = self.ring(
                    4096,
                    lambda f0=f0: w_up[li, :, f0:f0 + 512].rearrange("(k p) f -> p k f", p=128),
                    lambda d: d.rearrange("p (k f) -> p k f", k=8))
                for cc in range(4):
                    fl = jj * 4 + cc
                    for t in range(NT):
                        ts = self.tsl(t)
                        b = self.bank()
                        ps = self.PS[:, b, :]
                        for kc in range(KC):
                            self.mm(ps, wv[:, kc, cc * 128:(cc + 1) * 128], self.HN[:, kc, ts],
                                    kc == 0, kc == KC - 1, [wk, ("HN", kc, t)], [("ps", b)])
                        q = self.tmp_i % 3
                        self.tmp_i += 1
                        tmp = self.TMP[:, q, :]
                        self.act(tmp, ps, AF.Relu, [("ps", b)], [("TMP", q)])
                        self.dve(lambda e, o=A[:, fl, ts], tmp=tmp: e.tensor_tensor(o, tmp, tmp, ALU.mult),
                                 [("TMP", q)], [("A", fl, t)])
            for j2 in range(4):
                r0 = qd * 1024
                wv, wk = self.ring(
                    2048,
                    lambda r0=r0, j2=j2: w_dn[li, r0:r0 + 1024, j2 * 256:(j2 + 1) * 256].rearrange("(k p) f -> p k f", p=128),
                    lambda d: d.rearrange("p (k f) -> p k f", k=8))
                for mm_ in range(2):
                    m = j2 * 2 + mm_
                    for t in range(NT):
                        ts = self.tsl(t)
                        b = self.bank()
                        ps = self.PS[:, b, :]
                        for k in range(8):
                            self.mm(ps, wv[:, k, mm_ * 128:(mm_ + 1) * 128], A[:, k, ts],
                                    k == 0, k == 7, [wk, ("A", k, t)], [("ps", b)])
                        self.dve(lambda e, h=self.H[:, m, ts], ps=ps: e.tensor_tensor(h, ps, h, ALU.add),
                                 [("ps", b), ("H", m, t)], [("H", m, t)])

    def ple(self, li):
        sch = self.sch
        w_g = self.din("ple_w_gate", [DEPTH, D, D])
        w_p = self.din("ple_w_proj", [DEPTH, 256, D])
        pT = self.din("pT", [DEPTH, 256, T])
        sch.dma("pool", "ld_wpj",
                lambda e: e.dma_start(out=self.WPJ[:, :, :], in_=w_p[li].rearrange("(k p) f -> p k f", p=128)),
                writes=[("WPJ",)])
        for jj in range(2):
            wv, wk = self.ring(
                4096,
                lambda jj=jj: w_g[li, :, jj * 512:(jj + 1) * 512].rearrange("(k p) f -> p k f", p=128),
                lambda d: d.rearrange("p (k f) -> p k f", k=8))
            for t in range(NT):
                ts = self.tsl(t)
                pi = self.pt_i % 2
                self.pt_i += 1
                sch.dma("pool", f"ld_pt{pi}",
                        lambda e, pi=pi, ts=ts: e.dma_start(out=self.PT[:, pi, :, :],
                                                            in_=pT[li, :, ts].rearrange("(k p) t -> p k t", p=128)),
                        writes=[("PT", pi)])
                for cc in range(4):
                    m = jj * 4 + cc
                    b = self.bank()
                    ps = self.PS[:, b, :]
                    for kc in range(KC):
                        self.mm(ps, wv[:, kc, cc * 128:(cc + 1) * 128], self.HN[:, kc, ts],
                                kc == 0, kc == KC - 1, [wk, ("HN", kc, t)], [("ps", b)])
                    b2 = self.bank()
                    ps2 = self.PS[:, b2, :]
                    for k2 in range(2):
                        self.mm(ps2, self.WPJ[:, k2, m * 128:(m + 1) * 128], self.PT[:, pi, k2, :],
                                k2 == 0, k2 == 1, [("WPJ",), ("PT", pi)], [("ps", b2)])
                    q = self.tmp_i % 3
                    self.tmp_i += 1
                    g = self.TMP[:, q, :]
                    self.act(g, ps, AF.Sigmoid, [("ps", b)], [("TMP", q)])
                    q2 = self.tmp2_i % 2
                    self.tmp2_i += 1
                    t2 = self.TMP2[:, q2, :]
                    self.dve(lambda e, t2=t2, ps2=ps2, g=g: e.tensor_tensor(t2, ps2, g, ALU.mult),
                             [("ps", b2), ("TMP", q)], [("TMP2", q2)])
                    self.dve(lambda e, h=self.H[:, m, ts], t2=t2: e.tensor_tensor(h, t2, h, ALU.add),
                             [("TMP2", q2), ("H", m, t)], [("H", m, t)])

    def exchange(self, src_ap, dst_ap, ncols, dt, reads, writes, parts_in=None, parts_out=None):
        sch, nc = self.sch, self.nc
        i = self.xch_i
        self.xch_i += 1
        src = nc.dram_tensor(f"xs{i}", [128, ncols], dt)
        gat = nc.dram_tensor(f"xg{i}", [256, ncols], dt)
        ks, kg = (f"xs{i}",), (f"xg{i}",)
        if parts_in is None:
            parts_in = [(src_ap, lambda d: d)]
        if parts_out is None:
            parts_out = [(dst_ap, lambda d: d[0:128, :])]
        sch.dma("pool", "xa",
                [lambda e, a=a, f=f: e.dma_start(out=f(src.ap()), in_=a) for a, f in parts_in],
                reads=reads, writes=[ks])
        sch.dma("pool", "xc",
                lambda e: e.collective_compute("AllGather", ALU.bypass, replica_groups=PAIRS,
                                               ins=[src.ap().opt()], outs=[gat.ap().opt()]),
                reads=[ks], writes=[kg], inc=1)
        sch.dma("pool", "xb",
                [lambda e, a=a, f=f: e.dma_start(out=a, in_=f(gat.ap())) for a, f in parts_out],
                reads=[kg], writes=writes)

    def conv_mixer(self, j):
        sch = self.sch
        w_in = self.din("sc_w_in", [2, D, 3 * D])
        w_out = self.din("sc_w_out", [2, D, D])
        sch.phase(["Z", "ZT", "ZH"])
        ident = self.CM[:, M_ID:M_ID + 128]
        for k in range(3):
            for c in range(KC):
                col = C_SCW + (j * 3 + k) * 8 + c
                dg = self.DIAG[:, k * 8 + c, :]
                self.dve(lambda e, dg=dg, col=col: e.tensor_scalar(dg, ident, self.CF[:, col:col + 1], None, ALU.mult),
                         [("CM",), ("CF",)], [("DIAG", k * 8 + c)])
        ZW = T + 2
        Z = self.S[:, 0:8 * ZW].rearrange("p (k t) -> p k t", k=8)
        ZT = self.SMALL[:, 0:8].bitcast(BF16)
        ZH = self.SMALL[:, 8:16].bitcast(BF16)
        flag = self.CF[:, C_FLAG:C_FLAG + 1]
        w3 = w_in[j].rearrange("(k p) (s f) -> p k s f", p=128, s=3)
        for c in range(KC):
            wv, wk = self.ring(2048, lambda c=c: w3[:, :, 1:3, c * 128:(c + 1) * 128],
                               lambda d: d.rearrange("p (k s f) -> p k s f", k=8, s=2), nsplit=2)
            for t in range(NT):
                ts = self.tsl(t)
                bs = []
                for s_ in range(2):
                    b = self.bank()
                    bs.append(b)
                    for kc in range(KC):
                        self.mm(self.PS[:, b, :], wv[:, kc, s_, :], self.HN[:, kc, ts],
                                kc == 0, kc == KC - 1, [wk, ("HN", kc, t)], [("ps", b)])
                q = self.tmp_i % 3
                self.tmp_i += 1
                tmp = self.TMP[:, q, :]
                self.act(tmp, self.PS[:, bs[0], :], AF.Copy, [("ps", bs[0])], [("TMP", q)])
                self.dve(lambda e, o=Z[:, c, 2 + t * TT:2 + (t + 1) * TT], x=self.PS[:, bs[1], :], tmp=tmp:
                         e.tensor_tensor(o, x, tmp, ALU.mult),
                         [("ps", bs[1]), ("TMP", q)], [("Z", c, t)])
        self.dve(lambda e: e.tensor_copy(ZT.rearrange("p (c w) -> p c w", c=8), Z[:, :, T:T + 2]),
                 [("Z", c, NT - 1) for c in range(KC)], [("ZT",)])
        self.exchange(ZT, ZH, 16, BF16, [("ZT",)], [("ZH",)])
        self.dve(lambda e: e.tensor_scalar(Z[:, :, 0:2], ZH.rearrange("p (c w) -> p c w", c=8), flag, None, ALU.mult),
                 [("ZH",), ("CF",)], [("Z", c, "h") for c in range(KC)])
        gb = []
        for hh in range(2):
            gb.append(self.ring(4096, lambda hh=hh: w_in[j, :, hh * 512:(hh + 1) * 512].rearrange("(k p) f -> p k f", p=128),
                                lambda d: d.rearrange("p (k f) -> p k f", k=8)))
        for tgroup in ((3, 2, 1), (0,)):
            for c in range(KC):
                wv, wk = gb[c // 4]
                cl = c % 4
                for t in tgroup:
                    ts = self.tsl(t)
                    bg = self.bank()
                    for kc in range(KC):
                        self.mm(self.PS[:, bg, :], wv[:, kc, cl * 128:(cl + 1) * 128], self.HN[:, kc, ts],
                                kc == 0, kc == KC - 1, [wk, ("HN", kc, t)], [("ps", bg)])
                    bc = self.bank()
                    prev = ("Z", c, t - 1) if t > 0 else ("Z", c, "h")
                    for k in range(3):
                        self.mm(self.PS[:, bc, :], self.DIAG[:, k * 8 + c, :], Z[:, c, t * TT + k:t * TT + k + TT],
                                k == 0, k == 2, [("DIAG", k * 8 + c), ("Z", c, t), prev], [("ps", bc)])
                    q = self.tmp_i % 3
                    self.tmp_i += 1
                    tmp = self.TMP[:, q, :]
                    self.act(tmp, self.PS[:, bg, :], AF.Copy, [("ps", bg)], [("TMP", q)])
                    self.dve(lambda e, o=Z[:, c, 2 + t * TT:2 + (t + 1) * TT], x=self.PS[:, bc, :], tmp=tmp:
                             e.tensor_tensor(o, x, tmp, ALU.mult),
                             [("ps", bc), ("TMP", q)], [("Z", c, t)])
        for jj in range(2):
            wv, wk = self.ring(4096, lambda jj=jj: w_out[j, :, jj * 512:(jj + 1) * 512].rearrange("(k p) f -> p k f", p=128),
                               lambda d: d.rearrange("p (k f) -> p k f", k=8))
            for cc in range(4):
                m = jj * 4 + cc
                for t in range(NT):
                    ts = self.tsl(t)
                    b = self.bank()
                    ps = self.PS[:, b, :]
                    for kc in range(KC):
                        self.mm(ps, wv[:, kc, cc * 128:(cc + 1) * 128], Z[:, kc, 2 + t * TT:2 + (t + 1) * TT],
                                kc == 0, kc == KC - 1, [wk, ("Z", kc, t)], [("ps", b)])
                    self.dve(lambda e, h=self.H[:, m, ts], ps=ps: e.tensor_tensor(h, ps, h, ALU.add),
                             [("ps", b), ("H", m, t)], [("H", m, t)])

    def lru_mixer(self, j):
        sch = self.sch
        w_in = self.din("lru_w_in", [1, D, 2560])
        w_a = self.din("lru_w_a", [1, 10, 128, 128])
        w_x = self.din("lru_w_x", [1, 10, 128, 128])
        w_out = self.din("lru_w_out", [1, 1280, D])
        sch.phase(["G", "XR", "XB", "RA", "I", "M"])
        S = self.S
        ident = self.CM[:, M_ID:M_ID + 128]
        flag = self.CF[:, C_FLAG:C_FLAG + 1]
        SM = self.SMALL
        XT = SM[:, 16:36].bitcast(BF16)
        XH = SM[:, 36:56].bitcast(BF16)
        ONEC = SM[:, 61:62]
        HS = SM[:, 62:63]
        H0 = SM[:, 63:64]
        CN = SM[:, 0:10]
        self.dve(lambda e: e.memset(ONEC, 1.0), [], [("ONEC",)])
        for k in range(4):
            for n in range(10):
                col = C_LCW + k * 10 + n
                dg = self.DIAG[:, k * 10 + n, :]
                self.dve(lambda e, dg=dg, col=col: e.tensor_scalar(dg, ident, self.CF[:, col:col + 1], None, ALU.mult),
                         [("CM",), ("CF",)], [("DIAG", k * 10 + n)])
        self.act(CN, self.CF[:, C_LAM:C_LAM + 10], AF.Exp, [("CF",)], [("CN",)], scale=-1.0)
        self.act(CN, CN, AF.Ln, [("CN",), ("ONEC",)], [("CN",)], bias=ONEC)
        self.dve(lambda e: e.tensor_scalar(CN, CN, -8.0, None, ALU.mult), [("CN",)], [("CN",)])
        WA = self.PT[:, :, :, :].rearrange("p a b t -> p (a b t)")[:, 0:1280].rearrange("p (n k) -> p n k", n=10)
        WX = self.WPJ[:, :, :].rearrange("p a f -> p (a f)")[:, 0:1280].rearrange("p (n k) -> p n k", n=10)
        sch.dma("pool", "ld_pt0", lambda e: e.dma_start(out=WA, in_=w_a[0].rearrange("n j k -> j n k")),
                writes=[("PT", 0), ("PT", 1)])
        sch.dma("pool", "ld_wpj", lambda e: e.dma_start(out=WX, in_=w_x[0].rearrange("n j k -> j n k")),
                writes=[("WPJ",)])
        bh = self.bank(aux=True)
        for (c0, cw) in ((1280, 512), (1792, 512), (2304, 256)):
            wv, wk = self.ring(8 * cw, lambda c0=c0, cw=cw: w_in[0, :, c0:c0 + cw].rearrange("(k p) f -> p k f", p=128),
                               lambda d: d.rearrange("p (k f) -> p k f", k=8))
            for cl in range(cw // 128):
                n = (c0 - 1280) // 128 + cl
                for kc in range(KC):
                    self.mm(self.PS[:, bh, n * 4:(n + 1) * 4], wv[:, kc, cl * 128:(cl + 1) * 128], self.HN[:, kc, T - 4:T],
                            kc == 0, kc == KC - 1, [wk, ("HN", kc, NT - 1)], [("ps", bh)])
        self.act(XT, self.PS[:, bh, 0:40], AF.Copy, [("ps", bh)], [("XT",)])
        self.exchange(XT, XH, 40, BF16, [("XT",)], [("XH",)])
        w2 = w_in[0].rearrange("(k p) (s f) -> p k s f", p=128, s=2)
        Gs = [S[:, 0:2048], S[:, 2048:4096]]
        XRs = [S[:, 4096:6148], S[:, 6152:8204]]
        XBs = [S[:, 8208:10256], S[:, 10256:12304]]
        RA = S[:, 12304:16400].bitcast(F32)
        II = S[:, 16400:20496].bitcast(F32)
        MM = S[:, 20496:24592].bitcast(F32)

        def kk(p, par, t):
            return (p, par, t)

        def front(n):
            par = n % 2
            G, XR, XB = Gs[par], XRs[par], XBs[par]
            wv, wk = self.ring(2048, lambda n=n: w2[:, :, :, n * 128:(n + 1) * 128],
                               lambda d: d.rearrange("p (k s f) -> p k s f", k=8, s=2), nsplit=2)
            for t in range(NT):
                ts = self.tsl(t)
                bg = self.bank()
                for kc in range(KC):
                    self.mm(self.PS[:, bg, :], wv[:, kc, 0, :], self.HN[:, kc, ts], kc == 0, kc == KC - 1,
                            [wk, ("HN", kc, t)], [("ps", bg)])
                self.act(G[:, ts], self.PS[:, bg, :], AF.Gelu, [("ps", bg)], [kk("G", par, t)])
                bx = self.bank()
                for kc in range(KC):
                    self.mm(self.PS[:, bx, :], wv[:, kc, 1, :], self.HN[:, kc, ts], kc == 0, kc == KC - 1,
                            [wk, ("HN", kc, t)], [("ps", bx)])
                self.act(XR[:, 4 + t * TT:4 + (t + 1) * TT], self.PS[:, bx, :], AF.Copy, [("ps", bx)], [kk("XR", par, t)])
            self.dve(lambda e, n=n, XR=XR: e.tensor_scalar(XR[:, 0:4], XH[:, n * 4:(n + 1) * 4], flag, None, ALU.mult),
                     [("XH",), ("CF",)], [kk("XR", par, "h")])
            for t in range(NT):
                ts = self.tsl(t)
                bc = self.bank()
                prev = kk("XR", par, t - 1) if t > 0 else kk("XR", par, "h")
                for k in range(4):
                    self.mm(self.PS[:, bc, :], self.DIAG[:, k * 10 + n, :], XR[:, 1 + t * TT + k:1 + t * TT + k + TT],
                            k == 0, k == 3, [("DIAG", k * 10 + n), kk("XR", par, t), prev], [("ps", bc)])
                self.act(XB[:, ts], self.PS[:, bc, :], AF.Identity, [("ps", bc), ("CF",)], [kk("XB", par, t)],
                         bias=self.CF[:, C_LCB + n:C_LCB + n + 1])
                ba = self.bank()
                self.mm(self.PS[:, ba, :], WA[:, n, :], XB[:, ts], True, True, [("PT", 0), kk("XB", par, t)], [("ps", ba)])
                self.act(RA[:, ts], self.PS[:, ba, :], AF.Sigmoid, [("ps", ba), ("CF",)], [("RA", t)],
                         bias=self.CF[:, C_LBA + n:C_LBA + n + 1])
                bx2 = self.bank()
                self.mm(self.PS[:, bx2, :], WX[:, n, :], XB[:, ts], True, True, [("WPJ",), kk("XB", par, t)], [("ps", bx2)])
                self.act(II[:, ts], self.PS[:, bx2, :], AF.Sigmoid, [("ps", bx2), ("CF",)], [("I", t)],
                         bias=self.CF[:, C_LBX + n:C_LBX + n + 1])

        def mid(n):
            par = n % 2
            G, XB = Gs[par], XBs[par]
            allt = lambda p: [(p, t) for t in range(NT)]
            allp = lambda p: [(p, par, t) for t in range(NT)]
            self.act(RA, RA, AF.Exp, allt("RA") + [("CN",)], allt("RA"), scale=CN[:, n:n + 1])
            self.dve(lambda e: e.tensor_tensor(MM, RA, RA, ALU.mult), allt("RA"), [("M",)])
            self.act(MM, MM, AF.Sqrt, [("M",), ("ONEC",)], [("M",)], bias=ONEC, scale=-1.0)
            self.dve(lambda e, XB=XB: e.tensor_tensor(II, II, XB, ALU.mult), allt("I") + allp("XB"), allt("I"))
            self.dve(lambda e: e.tensor_tensor(II, II, MM, ALU.mult), allt("I") + [("M",)], allt("I"))
            self.dve(lambda e: e.tensor_tensor_scan(MM, RA, II, 0.0, ALU.mult, ALU.add),
                     allt("RA") + allt("I") + [("M",)], [("M",)])
            self.dve(lambda e: e.tensor_copy(HS, MM[:, T - 1:T]), [("M",)], [("HS",)])
            self.dve(lambda e: e.tensor_scalar(II, RA, 0.0, None, ALU.mult), allt("RA") + allt("I"), allt("I"))
            self.dve(lambda e: e.tensor_tensor_scan(II, RA, II, 1.0, ALU.mult, ALU.add),
                     allt("RA") + allt("I"), allt("I"))
            self.exchange(HS, H0, 1, F32, [("HS",)], [("H0",)])
            self.dve(lambda e, XB=XB, G=G: e.tensor_tensor(XB, II, G, ALU.mult), allt("I") + allp("G") + allp("XB"), allp("XB"))
            self.dve(lambda e, G=G: e.tensor_tensor(G, MM, G, ALU.mult), [("M",)] + allp("G"), allp("G"))

        def back(n):
            par = n % 2
            G, XB = Gs[par], XBs[par]
            allp = lambda p: [(p, par, t) for t in range(NT)]
            self.dve(lambda e: e.tensor_scalar(H0, H0, flag, None, ALU.mult), [("H0",), ("CF",)], [("H0",)])
            self.dve(lambda e, XB=XB, G=G: e.scalar_tensor_tensor(G, XB, H0, G, ALU.mult, ALU.add),
                     allp("XB") + allp("G") + [("H0",)], allp("G"))
            wo, wok = self.ring(1024, lambda n=n: w_out[0, n * 128:(n + 1) * 128, :], lambda d: d)
            for m in range(KC):
                for t in range(NT):
                    ts = self.tsl(t)
                    b = self.bank()
                    self.mm(self.PS[:, b, :], wo[:, m * 128:(m + 1) * 128], G[:, ts], True, True,
                            [wok, ("G", par, t)], [("ps", b)])
                    self.dve(lambda e, h=self.H[:, m, ts], ps=self.PS[:, b, :]: e.tensor_tensor(h, ps, h, ALU.add),
                             [("ps", b), ("H", m, t)], [("H", m, t)])

        if LRU_PIPE:
            front(0)
            for n in range(10):
                mid(n)
                if n + 1 < 10:
                    front(n + 1)
                back(n)
        else:
            for n in range(10):
                front(n)
                mid(n)
                back(n)

    def rope_tables(self):
        S = self.S
        pos = self.din("pos", [32, T], I32)
        PI = S[0:32, 0:4096].bitcast(I32)
        ANG = S[0:32, 4096:8192].bitcast(F32)
        KF = S[0:32, 8192:12288].bitcast(F32)
        ANC = S[0:32, 12288:16384].bitcast(F32)
        COS = self.TMP2[0:32, :, :].rearrange("p a t -> p (a t)").bitcast(BF16)
        SIN = self.RSTD[0:32, :, :].rearrange("p a t -> p (a t)").bitcast(BF16)
        tk = [("TMP2", 0), ("TMP2", 1)]
        rk = [("RSTD", 0), ("RSTD", 1)]
        self.sch.dma("sp", "ld_pos", lambda e: e.dma_start(out=PI, in_=pos), writes=[("RT",)])
        self.dve(lambda e: e.tensor_copy(ANG, PI), [("RT",)], [("RT",)])
        self.dve(lambda e: e.tensor_scalar(ANG, ANG, self.CF[0:32, C_INVF:C_INVF + 1], None, ALU.mult),
                 [("RT",), ("CF",)], [("RT",)])
        MAGIC = 12582912.0
        TWO_PI = 6.283185307179586
        PI_ = 3.1415925
        for which, dst, dk in ((0, SIN, rk), (1, COS, tk)):
            src = ANG
            if which == 1:
                self.dve(lambda e: e.tensor_scalar(ANC, ANG, 1.5707963267948966, None, ALU.add), [("RT",)], [("RT",)])
                src = ANC
            self.dve(lambda e, src=src: e.tensor_scalar(KF, src, 1.0 / TWO_PI, MAGIC, ALU.mult, ALU.add),
                     [("RT",), ("RT",)], [("RT",)])
            self.dve(lambda e: e.tensor_scalar(KF, KF, -MAGIC, -TWO_PI, ALU.add, ALU.mult), [("RT",)], [("RT",)])
            self.dve(lambda e, src=src: e.tensor_tensor(KF, KF, src, ALU.add), [("RT",), ("RT",), ("RT",)], [("RT",)])
            self.dve(lambda e: e.tensor_scalar(KF, KF, -PI_, PI_, ALU.max, ALU.min), [("RT",)], [("RT",)])
            self.act(dst, KF, AF.Sin, [("RT",)], dk)
        return COS, SIN, tk, rk

    def attn_mixer(self, j):
        sch = self.sch
        w_qkv = self.din("attn_w_qkv", [1, D, 9216])
        w_o = self.din("attn_w_o", [1, D, D])
        sch.phase(["RT"])
        S = self.S
        COS, SIN, tk, rk = self.rope_tables()
        sch.phase(["KT", "V", "QT", "VT", "ND"])
        KT = S[:, 0:6144].rearrange("p (g t) -> p g t", g=3)
        V = S[:, 6144:12288].rearrange("p (b d) -> p b d", d=128)
        QT = S[:, 12288:14336]
        VT = S[:, 14336:16384]
        ND = S[:, 16384:24576].bitcast(F32).rearrange("p (a t) -> p a t", a=2)
        HK = self.DIAG[:, 0:21, :]
        HVa = self.WPJ[:, :, :].rearrange("p a (b d) -> p (a b) d", d=128)
        HVb = self.PT[:, :, :, :].rearrange("p a b t -> p (a b t)")[:, 0:640].rearrange("p (b d) -> p b d", d=128)
        ident = self.CM[:, M_ID:M_ID + 128]
        ones = self.CM[:, M_ONES:M_ONES + 128]
        rot = self.CM[:, M_ROT128:M_ROT128 + 128]
        RT = [self.TMP[:, i, :].bitcast(BF16) for i in range(3)] + \
             [self.DIAG[:, 21 + 8 * i:29 + 8 * i, :].rearrange("p a b -> p (a b)") for i in range(2)]
        dummy = self.SMALL[:, 59:60]
        self.dve(lambda e: e.memset(dummy, 0.0), [], [("SQ", i) for i in range(3)] + [("P", i) for i in range(6)])
        for i in range(3):
            self.dve(lambda e, i=i: e.memset(RT[i], 0.0), [("TMP", i)], [("RTB", i), ("TMP", i)])
        for i in range(2):
            self.dve(lambda e, i=i: e.memset(RT[3 + i], 0.0), [("DIAG", 21 + 8 * i + a) for a in range(8)],
                     [("RTB", 3 + i)] + [("DIAG", 21 + 8 * i + a) for a in range(8)])
        DIL = (1, 4, 16)
        SCALE = 128.0 ** -0.5
        wq4 = w_qkv[0].rearrange("(k p) (s g h f) -> p k s g h f", p=128, s=3, g=3, h=8)

        def hv(i):
            return (HVa[:, i, :], ("WPJ",)) if i < 16 else (HVb[:, i - 16, :], ("PT", 0))

        def perm_out(dst2d, t, d):
            v = dst2d.rearrange("p (r m) -> p m r", r=d)
            return v[:, t * (TT // d):(t + 1) * (TT // d), :]

        def project(wslice, wk, dst2d, dkey, d, rope):
            for t in range(NT):
                ts = self.tsl(t)
                b = self.bank()
                ps = self.PS[:, b, :]
                for kc in range(KC):
                    self.mm(ps, wslice[:, kc, :], self.HN[:, kc, ts], kc == 0, kc == KC - 1,
                            [wk, ("HN", kc, t)], [("ps", b)])
                pin = ps.rearrange("p (m r) -> p m r", r=d)
                self.act(perm_out(dst2d, t, d), pin, AF.Copy, [("ps", b)], [dkey])
                if rope:
                    qi = self.rt_i % 5; self.rt_i += 1
                    U = RT[qi]
                    self.dve(lambda e, U=U, ps=ps, ts=ts: e.tensor_tensor(U[0:32, 0:TT], ps[0:32, :], COS[:, ts], ALU.mult),
                             [("ps", b)] + tk, [("RTB", qi)])
                    self.dve(lambda e, U=U, ps=ps, ts=ts: e.tensor_tensor(U[0:32, TT:2 * TT], ps[0:32, :], SIN[:, ts], ALU.mult),
                             [("ps", b)] + rk, [("RTB", qi)])
                    b2 = self.bank()
                    self.mm(self.PS[:, b2, :], ident, U[:, 0:TT], True, False, [("RTB", qi), ("CM",)], [("ps", b2)])
                    self.mm(self.PS[:, b2, :], rot, U[:, TT:2 * TT], False, True, [("RTB", qi), ("CM",)], [("ps", b2)])
                    self.act(perm_out(dst2d[0:32, :], t, d), self.PS[0:32, b2, :].rearrange("p (m r) -> p m r", r=d),
                             AF.Copy, [("ps", b2), dkey], [dkey])

        if ATT_STAGE == 0:
            return
        for h in range(8 if ATT_STAGE == 9 else 1):
            for g in range(3):
                d = DIL[g]
                wv, wk = self.ring(2048, lambda g=g, h=h: wq4[:, :, 1:3, g, h, :],
                                   lambda dd: dd.rearrange("p (k s f) -> p k s f", k=8, s=2), nsplit=2)
                project(wv[:, :, 0, :], wk, KT[:, g, :], ("KT", g), d, ATT_STAGE >= 0.7)
                project(wv[:, :, 1, :], wk, VT, ("VT",), d, False)
                for bq in range(4 if ATT_STAGE >= 1 else 0):
                    b = self.bank()
                    psb = self.PS[:, b, 0:256].bitcast(BF16)
                    for i in range(4):
                        blk = bq * 4 + i
                        self.sch.op("pe", lambda e, o=psb[:, i * 128:(i + 1) * 128], a=VT[:, blk * 128:(blk + 1) * 128]:
                                    e.transpose(o, a, ident), [("VT",), ("CM",)], [("ps", b)])
                    self.act(V[:, g * 16 + bq * 4:g * 16 + bq * 4 + 4, :],
                             psb.rearrange("p (i d) -> p i d", d=128), AF.Copy, [("ps", b)], [("V", g)])
            qw = [self.ring(1024, lambda g=g, h=h: wq4[:, :, 0, g, h, :],
                            lambda dd: dd.rearrange("p (k f) -> p k f", k=8)) for g in range(3)]
            kt1 = KT[:, 1, :].rearrange("p (r b c) -> p r b c", r=4, b=4)[:, :, 3, :]
            v1 = V[:, 16:32, :].rearrange("p (r b) d -> p r b d", r=4)[:, :, 3, :]
            NB = 128
            parts_in = [
                (KT[:, 0, 15 * NB:16 * NB], lambda dd: dd[:, 0:NB]),
                (kt1, lambda dd: dd[:, NB:5 * NB].rearrange("p (r c) -> p r c", r=4)),
                (KT[:, 2, :], lambda dd: dd[:, 5 * NB:21 * NB]),
                (V[:, 15, :], lambda dd: dd[:, 21 * NB:22 * NB]),
                (v1, lambda dd: dd[:, 22 * NB:26 * NB].rearrange("p (r c) -> p r c", r=4)),
                (V[:, 32:48, :], lambda dd: dd[:, 26 * NB:42 * NB].rearrange("p (b c) -> p b c", b=16)),
            ]
            parts_out = [
                (HK, lambda gg: gg[0:128, 0:21 * NB].rearrange("p (b c) -> p b c", b=21)),
                (HVa, lambda gg: gg[0:128, 21 * NB:37 * NB].rearrange("p (b c) -> p b c", b=16)),
                (HVb, lambda gg: gg[0:128, 37 * NB:42 * NB].rearrange("p (b c) -> p b c", b=5)),
            ]
            self.exchange(None, None, 42 * NB, BF16,
                          [("KT", 0), ("KT", 1), ("KT", 2), ("V", 0), ("V", 1), ("V", 2)],
                          [("DIAG", i) for i in range(21)] + [("WPJ",), ("PT", 0), ("PT", 1)],
                          parts_in=parts_in, parts_out=parts_out)
            for g in range(3):
                d = DIL[g]
                nb = 16 // d
                wv, wk = qw[g]
                project(wv, wk, QT, ("QT",), d, True)
                for jb in [x for x in range(16) if x % nb != 0] + [x for x in range(16) if x % nb == 0]:
                    r, jl = jb // nb, jb % nb
                    first = jl == 0
                    qs = QT[:, jb * 128:(jb + 1) * 128]
                    if first:
                        hb = (0, 1 + r, 5 + r)[g]
                        kprev, kpk = HK[:, hb, :], ("DIAG", hb)
                        vprev, vpk = hv(hb)
                        mask = self.CM[:, M_MASKH:M_MASKH + 256]
                    else:
                        kprev, kpk = KT[:, g, (jb - 1) * 128:jb * 128], ("KT", g)
                        vprev, vpk = V[:, g * 16 + jb - 1, :], ("V", g)
                        mask = self.CM[:, M_MASK:M_MASK + 256]
                    b = self.bank()
                    ps = self.PS[:, b, 0:256]
                    self.mm(ps[:, 0:128], kprev, qs, True, True, [kpk, ("QT",)], [("ps", b)])
                    self.mm(ps[:, 128:256], KT[:, g, jb * 128:(jb + 1) * 128], qs, True, True,
                            [("KT", g), ("QT",)], [("ps", b)])
                    q = self.p_i % 6; self.p_i += 1
                    P = self.SQ[:, q // 2, (q % 2) * 256:(q % 2) * 256 + 256]
                    pk = ("P", q)
                    self.act(P, ps, AF.Exp, [("ps", b), ("SQ", q // 2)], [pk], scale=SCALE)
                    self.dve(lambda e, P=P, mask=mask: e.tensor_tensor(P, P, mask, ALU.mult),
                             [pk, ("CM",)], [pk])
                    b2 = self.bank()
                    po = self.PS[:, b2, 0:256]
                    self.mm(po[:, 0:128], vprev, P[:, 0:128], True, False, [vpk, pk], [("ps", b2)])
                    self.mm(po[:, 0:128], V[:, g * 16 + jb, :], P[:, 128:256], False, True, [("V", g), pk], [("ps", b2)])
                    self.mm(po[:, 128:256], ones, P[:, 0:128], True, False, [("CM",), pk], [("ps", b2)])
                    self.mm(po[:, 128:256], ones, P[:, 128:256], False, True, [("CM",), pk], [("ps", b2)])
                    st = jl * 128 * d + r
                    dst = ND[:, :, st:st + 127 * d + 1:d]
                    pin = po.rearrange("p (a c) -> p a c", a=2)
                    if g == 0:
                        self.dve(lambda e, dst=dst, pin=pin: e.tensor_copy(dst, pin), [("ps", b2)], [("ND",)])
                    else:
                        self.dve(lambda e, dst=dst, pin=pin: e.tensor_tensor(dst, pin, dst, ALU.add),
                                 [("ps", b2), ("ND",)], [("ND",)])
            self.dve(lambda e: e.reciprocal(ND[:, 1, :], ND[:, 1, :]), [("ND",)], [("ND",)])
            self.dve(lambda e: e.tensor_tensor(VT, ND[:, 0, :], ND[:, 1, :], ALU.mult), [("ND",), ("VT",)], [("VT",)])
            wo, wok = self.ring(1024, lambda h=h: w_o[0, h * 128:(h + 1) * 128, :], lambda dd: dd)
            for m in range(KC):
                for t in range(NT):
                    ts = self.tsl(t)
                    b = self.bank()
                    self.mm(self.PS[:, b, :], wo[:, m * 128:(m + 1) * 128], VT[:, ts], True, True,
                            [wok, ("VT",)], [("ps", b)])
                    self.dve(lambda e, hh=self.H[:, m, ts], ps=self.PS[:, b, :]: e.tensor_tensor(hh, ps, hh, ALU.add),
                             [("ps", b), ("H", m, t)], [("H", m, t)])
        self.dve(lambda e: e.memset(dummy, 0.0), [],
                 [("SQ", i) for i in range(3)] + [("P", i) for i in range(6)] + [("RTB", i) for i in range(5)] +
                 [("TMP", i) for i in range(3)] + [("DIAG", 21 + a) for a in range(16)])


_CACHE = {}


def _const_tables(flag):
    cm = np.zeros((128, NCM), np.float32)
    cm[:, M_ID:M_ID + 128] = np.eye(128, dtype=np.float32)
    cm[:, M_ONESD:M_ONESD + 128] = 1.0 / 1024.0
    cm[:, M_ONES:M_ONES + 128] = 1.0
    for m in range(16):
        cm[m + 16, M_ROT + m] = -1.0
        cm[m, M_ROT + 16 + m] = 1.0
        cm[m + 16, M_ROT128 + m] = -1.0
        cm[m, M_ROT128 + 16 + m] = 1.0
    k = np.arange(128)[:, None]
    q = np.arange(128)[None, :]
    maskP = (k >= q).astype(np.float32)
    maskO = (k <= q).astype(np.float32)
    cm[:, M_MASK:M_MASK + 128] = maskP
    cm[:, M_MASK + 128:M_MASK + 256] = maskO
    cm[:, M_MASKH:M_MASKH + 128] = maskP * flag
    cm[:, M_MASKH + 128:M_MASKH + 256] = maskO
    return cm


def _cols(v):
    v = np.asarray(v, np.float32)
    return np.ascontiguousarray(v.reshape(-1, 128).T)


def _cf_table(inp, flag):
    cf = np.zeros((128, NCF), np.float32)
    for l in range(DEPTH):
        cf[:, C_NMIX + l * 8:C_NMIX + l * 8 + 8] = _cols(inp["norm_mix"][l])
        cf[:, C_NMLP + l * 8:C_NMLP + l * 8 + 8] = _cols(inp["norm_mlp"][l])
        cf[:, C_NPLE + l * 8:C_NPLE + l * 8 + 8] = _cols(inp["norm_ple"][l])
    cf[:, C_NFIN:C_NFIN + 8] = _cols(inp["norm_final"])
    for jj in range(2):
        for k in range(3):
            c0 = C_SCW + (jj * 3 + k) * 8
            cf[:, c0:c0 + 8] = _cols(inp["sc_w_conv"][jj, k])
    for k in range(4):
        cf[:, C_LCW + k * 10:C_LCW + k * 10 + 10] = _cols(inp["lru_conv_w"][0, k])
    cf[:, C_LCB:C_LCB + 10] = _cols(inp["lru_conv_b"][0])
    cf[:, C_LBA:C_LBA + 10] = _cols(inp["lru_b_a"][0])
    cf[:, C_LBX:C_LBX + 10] = _cols(inp["lru_b_x"][0])
    cf[:, C_LAM:C_LAM + 10] = _cols(inp["lru_lambda"][0])
    cf[:, C_FLAG] = flag
    half = 16
    inv = (500000.0 ** (-2.0 * np.arange(half, dtype=np.float32) / 32.0)).astype(np.float32)
    cf[0:16, C_INVF] = inv
    cf[16:32, C_INVF] = inv
    return cf


WEIGHT_NAMES = ["sc_w_in", "sc_w_out", "attn_w_qkv", "attn_w_o", "lru_w_in", "lru_w_a", "lru_w_x",
                "lru_w_out", "mlp_w_up", "mlp_w_down", "ple_w_gate", "ple_w_proj"]


def run_layers(inp, hT_per_core, layers, final_norm):
    key = (tuple(layers), final_norm)
    if key not in _CACHE:
        prog = Prog(list(layers), final_norm)
        nc = prog.build()
        _CACHE[key] = (prog, nc)
    prog, nc = _CACHE[key]
    in_maps = []
    for c in range(N_CORES):
        b, half = c // 2, c % 2
        tok = slice(half * T, (half + 1) * T)
        m = {}
        for name in prog.in_names:
            if name == "cf":
                m[name] = _cf_table(inp, float(half))
            elif name == "cm":
                m[name] = _const_tables(float(half))
            elif name == "xT":
                m[name] = np.ascontiguousarray(hT_per_core[c], dtype=np.float32)
            elif name == "pT":
                m[name] = np.ascontiguousarray(np.transpose(inp["p"][:, b, tok, :], (0, 2, 1)), dtype=np.float32)
            elif name == "pos":
                m[name] = np.ascontiguousarray(np.broadcast_to(inp["positions"][b, tok].reshape(1, T), (32, T)), dtype=np.int32)
            else:
                m[name] = np.ascontiguousarray(inp[name], dtype=np.float32)
        in_maps.append(m)
    res = run_bass_kernel_spmd(nc, in_maps, core_ids=list(range(N_CORES)))
    return [res.results[c]["outT"] for c in range(N_CORES)]


LAUNCH_PLAN = [([0, 1, 2, 3], True)]


def kernel(**inputs):
    inp = {k: np.asarray(v) for k, v in inputs.items()}
    x = inp["x"]
    hT = []
    for c in range(N_CORES):
        b, half = c // 2, c % 2
        hT.append(np.ascontiguousarray(x[b, half * T:(half + 1) * T, :].T))
    for layers, fin in LAUNCH_PLAN:
        hT = run_layers(inp, hT, layers, fin)
    out = np.empty((4, 2 * T, D), np.float32)
    for c in range(N_CORES):
        b, half = c // 2, c % 2
        out[b, half * T:(half + 1) * T, :] = hT[c].T
    return out
```

```python
import numpy as np
from contextlib import ExitStack
import concourse.bass as bass
import concourse.mybir as mybir
from concourse.bass_utils import run_bass_kernel_spmd

F32, BF16, I32 = mybir.dt.float32, mybir.dt.bfloat16, mybir.dt.int32
AF = mybir.ActivationFunctionType
ALU = mybir.AluOpType

T = 2048
TT = 512
NT = T // TT
D = 1024
KC = 8
DEPTH = 4
EPS = 1e-6
N_CORES = 8
PAIRS = [[0, 1], [2, 3], [4, 5], [6, 7]]

C_NMIX = 0
C_NMLP = 32
C_NPLE = 64
C_NFIN = 96
C_SCW = 104
C_LCW = 152
C_LCB = 192
C_LBA = 202
C_LBX = 212
C_LAM = 222
C_FLAG = 232
C_INVF = 233
NCF = 234
M_ID = 0
M_ONESD = 128
M_ONES = 256
M_ROT = 384
M_MASK = 416
M_MASKH = 672
M_ROT128 = 928
NCM = 1056

S_BYTES = 50176
MIXER_ONLY = False
BANKS8 = False
BLOCK_LOOKAHEAD = 2
LRU_PIPE = False
ATT_STAGE = 9


class Sched:
    ENG = ("pe", "act", "dve", "pool", "sp")

    def __init__(self, nc, es):
        self.nc, self.es = nc, es
        self.streams = {e: [] for e in self.ENG}
        self.cnt, self.sem = {}, {}
        self.seen = {e: {} for e in self.ENG}
        self.res = {}
        self.inherit = {}
        self.live_prefixes = set()
        for e in ("pe", "act", "dve"):
            self.new_sem(e)

    def new_sem(self, name):
        self.sem[name] = self.es.enter_context(self.nc.semaphore(name))
        self.cnt[name] = 0

    def _get(self, k):
        r = self.res.get(k)
        if r is None:
            inh = self.inherit.get(k[0])
            if inh:
                return [None, inh]
        return r

    def _deps(self, reads, writes):
        need = {}

        def add(s, v):
            if need.get(s, 0) < v:
                need[s] = v
        for k in reads:
            r = self._get(k)
            if r and r[0]:
                add(*r[0])
            if r and k[0] == "ps":
                for s, v in r[1].items():
                    add(s, v)
        for k in writes:
            r = self._get(k)
            if r:
                if r[0]:
                    add(*r[0])
                for s, v in r[1].items():
                    add(s, v)
        return need

    def _waits(self, eng, need):
        for s, v in need.items():
            if s == "pe" and eng == "pe":
                continue
            if self.seen[eng].get(s, 0) >= v:
                continue
            self.seen[eng][s] = v
            self.streams[eng].append(("wait", s, v))

    def _mark(self, reads, writes, sv):
        for k in reads:
            r = self.res.get(k)
            if r is None:
                inh = self.inherit.get(k[0])
                r = self.res[k] = [None, dict(inh) if inh else {}]
            if r[1].get(sv[0], 0) < sv[1]:
                r[1][sv[0]] = sv[1]
        for k in writes:
            self.res[k] = [sv, {}]

    def op(self, eng, fn, reads=(), writes=()):
        self._waits(eng, self._deps(reads, writes))
        self.cnt[eng] += 1
        self.streams[eng].append(("op", fn, eng, 1))
        self._mark(reads, writes, (eng, self.cnt[eng]))

    def dma(self, queue, slot, fn, reads=(), writes=(), inc=16):
        fns = fn if isinstance(fn, (list, tuple)) else [fn]
        if slot not in self.sem:
            self.new_sem(slot)
        need = self._deps(reads, writes)
        if self.cnt[slot] > 0:
            need[slot] = max(need.get(slot, 0), self.cnt[slot])
        self._waits(queue, need)
        for f in fns:
            self.cnt[slot] += inc
            self.streams[queue].append(("op", f, slot, inc))
        self._mark(reads, writes, (slot, self.cnt[slot]))

    def wait_all(self, queue, keys):
        self._waits(queue, self._deps(keys, ()))

    def phase(self, prefixes):
        need = {}
        for k in list(self.res.keys()):
            if k[0] in self.live_prefixes:
                r = self.res.pop(k)
                if r[0] and need.get(r[0][0], 0) < r[0][1]:
                    need[r[0][0]] = r[0][1]
                for s, v in r[1].items():
                    if need.get(s, 0) < v:
                        need[s] = v
        for p in self.live_prefixes:
            for s, v in self.inherit.get(p, {}).items():
                if need.get(s, 0) < v:
                    need[s] = v
            self.inherit.pop(p, None)
        self.live_prefixes = set(prefixes)
        for p in prefixes:
            self.inherit[p] = dict(need)

    def replay(self, block):
        def mk(name):
            def body(e):
                for it in self.streams[name]:
                    if it[0] == "wait":
                        e.wait_ge(self.sem[it[1]], it[2])
                    else:
                        it[1](e).then_inc(self.sem[it[2]], it[3])
            return body
        block.tensor(mk("pe"))
        block.scalar(mk("act"))
        block.vector(mk("dve"))
        block.gpsimd(mk("pool"))
        block.sync(mk("sp"))


class Prog:
    def __init__(self, layers, final_norm, load_h_name="xT"):
        self.layers = layers
        self.final_norm = final_norm
        self.nc = nc = bass.Bass("TRN2", target_bir_lowering=False)
        self.dram = {}
        self.es = ExitStack()
        self.in_names = []
        self.load_h_name = load_h_name

    def din(self, name, shape, dt=F32):
        if name not in self.dram:
            self.dram[name] = self.nc.dram_tensor(name, list(shape), dt, kind="ExternalInput").ap()
            self.in_names.append(name)
        return self.dram[name]

    def sb(self, name, shape, dt):
        return self.es.enter_context(self.nc.sbuf_tensor(name, list(shape), dt))

    def build(self):
        nc, es = self.nc, self.es
        with es:
            self.out = nc.dram_tensor("outT", [D, T], F32, kind="ExternalOutput").ap()
            self.H = self.sb("H", [128, KC, T], F32)
            self.HN = self.sb("HN", [128, KC, T], BF16)
            self.S = self.sb("S", [128, S_BYTES // 2], BF16)
            self.RING = self.sb("RING", [128, 3, 4096], BF16)
            self.PT = self.sb("PT", [128, 2, 2, TT], BF16)
            self.WPJ = self.sb("WPJ", [128, 2, D], BF16)
            self.SQ = self.sb("SQ", [128, 3, TT], BF16)
            self.RSTD = self.sb("RSTD", [128, 2, TT], F32)
            self.TMP = self.sb("TMP", [128, 3, TT], F32)
            self.TMP2 = self.sb("TMP2", [128, 2, TT], F32)
            self.DIAG = self.sb("DIAG", [128, 40, 128], BF16)
            self.CF = self.sb("CF", [128, NCF], F32)
            self.CM = self.sb("CM", [128, NCM], BF16)
            self.SMALL = self.sb("SMALL", [128, 64], F32)
            self.PS = es.enter_context(nc.psum_tensor("PS", [128, 8, TT], F32))
            self.sch = Sched(nc, es)
            self.ring_i = 0
            self.bank_i = 0
            self.tmp_i = 0
            self.tmp2_i = 0
            self.sq_i = 0
            self.rstd_i = 0
            self.pt_i = 0
            self.xch_i = 0
            self.rt_i = 0
            self.p_i = 0
            self.emit()
            with nc.Block() as block:
                self.sch.replay(block)
        return nc

    def bank(self, pool=None, aux=False):
        if pool is None:
            pool = (0, 1, 2, 3, 4, 5, 6, 7) if BANKS8 else ((6, 7) if aux else (0, 1, 2, 3, 4, 5))
        b = pool[self.bank_i % len(pool)]
        self.bank_i += 1
        return b

    def ring(self, n_elems, src_ap_fn, view_fn, reads=(), nsplit=0):
        s = self.ring_i % 3
        self.ring_i += 1
        dst_flat = self.RING[:, s, 0:n_elems]
        view = view_fn(dst_flat)
        key = ("ring", s)
        if nsplit:
            fns = [lambda e, v=view, a=src_ap_fn, i=i: e.dma_start(out=v[:, :, i, :], in_=a()[:, :, i, :])
                   for i in range(nsplit)]
        else:
            fns = [lambda e, v=view, a=src_ap_fn: e.dma_start(out=v, in_=a())]
        self.sch.dma("pool", f"ring{s}", fns, reads=reads, writes=[key])
        return view, key

    def mm(self, out, lhsT, rhs, start, stop, reads, writes):
        self.sch.op("pe", lambda e: e.matmul(out, lhsT, rhs, start=start, stop=stop), reads, writes)

    def act(self, out, in_, func, reads, writes, bias=None, scale=None):
        kw = {}
        if bias is not None:
            kw["bias"] = bias
        if scale is not None:
            kw["scale"] = scale
        self.sch.op("act", lambda e: e.activation(out=out, in_=in_, func=func, **kw), reads, writes)

    def dve(self, fn, reads, writes):
        self.sch.op("dve", fn, reads, writes)

    def tsl(self, t):
        return slice(t * TT, (t + 1) * TT)

    def emit(self):
        sch = self.sch
        cf = self.din("cf", [128, NCF])
        cm = self.din("cm", [128, NCM])
        xT = self.din(self.load_h_name, [D, T])
        sch.dma("sp", "ld_cf", lambda e: e.dma_start(out=self.CF[:, :], in_=cf), writes=[("CF",)])
        sch.dma("pool", "ld_cm", lambda e: e.dma_start(out=self.CM[:, :], in_=cm), writes=[("CM",)])
        for c in range(KC):
            sch.dma("sp", f"ld_h{c % 4}",
                    lambda e, c=c: e.dma_start(out=self.H[:, c, :], in_=xT[c * 128:(c + 1) * 128, :]),
                    writes=[("H", c, t) for t in range(NT)])
        self.EPSC = self.SMALL[:, 60:61]
        self.dve(lambda e: e.memset(self.EPSC, EPS), [], [("EPSC",)])
        for li in self.layers:
            kind, j = li % 3, li // 3
            self.rmsnorm(C_NMIX + li * 8)
            if kind == 0:
                self.conv_mixer(j)
            elif kind == 1:
                self.attn_mixer(j)
            else:
                self.lru_mixer(j)
            if MIXER_ONLY:
                continue
            self.rmsnorm(C_NMLP + li * 8)
            self.mlp(li)
            self.rmsnorm(C_NPLE + li * 8)
            self.ple(li)
        if self.final_norm:
            self.rmsnorm(C_NFIN, final=True)
        for c in range(KC):
            sch.dma("sp", f"st{c % 4}",
                    lambda e, c=c: e.dma_start(out=self.out[c * 128:(c + 1) * 128, :], in_=self.H[:, c, :]),
                    reads=[("H", c, t) for t in range(NT)])
        sch.wait_all("sp", [("H", c, t) for c in range(KC) for t in range(NT)])
        for s in range(4):
            name = f"st{s}"
            sch.streams["sp"].append(("wait", name, sch.cnt[name]))

    def rmsnorm(self, gcol, final=False, tiles=None):
        sch = self.sch
        onesD = self.CM[:, M_ONESD:M_ONESD + 128]
        for t in (tiles or range(NT)):
            ts = self.tsl(t)
            b = self.bank(aux=True)
            ps = self.PS[:, b, :]
            for c in range(KC):
                q = self.sq_i % 3
                self.sq_i += 1
                sq = self.SQ[:, q, :]
                self.act(sq, self.H[:, c, ts], AF.Square, [("H", c, t)], [("SQ", q)])
                self.mm(ps, onesD, sq, c == 0, c == KC - 1, [("SQ", q), ("CM",)], [("ps", b)])
            r = self.rstd_i % 2
            self.rstd_i += 1
            rstd = self.RSTD[:, r, :]
            self.act(rstd, ps, AF.Sqrt, [("ps", b), ("EPSC",)], [("RSTD", r)], bias=self.EPSC)
            self.dve(lambda e, rstd=rstd: e.reciprocal(rstd, rstd), [("RSTD", r)], [("RSTD", r)])
            for c in range(KC):
                g = self.CF[:, gcol + c:gcol + c + 1]
                if final:
                    out = self.H[:, c, ts]
                    wk = [("H", c, t)]
                else:
                    out = self.HN[:, c, ts]
                    wk = [("HN", c, t)]
                self.dve(lambda e, out=out, c=c, ts=ts, g=g, rstd=rstd:
                         e.scalar_tensor_tensor(out, self.H[:, c, ts], g, rstd, ALU.mult, ALU.mult),
                         [("H", c, t), ("RSTD", r), ("CF",)], wk)

    def mlp(self, li):
        sch = self.sch
        w_up = self.din("mlp_w_up", [DEPTH, D, 4 * D])
        w_dn = self.din("mlp_w_down", [DEPTH, 4 * D, D])
        sch.phase(["A"])
        A = self.S[:, 0:8 * T].rearrange("p (k t) -> p k t", k=8)
        for qd in range(4):
            for jj in range(2):
                f0 = qd * 1024 + jj * 512
                wv, wk = self.ring(
                    4096,
                    lambda f0=f0: w_up[li, :, f0:f0 + 512].rearrange("(k p) f -> p k f", p=128),
                    lambda d: d.rearrange("p (k f) -> p k f", k=8))
                for cc in range(4):
                    fl = jj * 4 + cc
                    for t in range(NT):
                        ts = self.tsl(t)
                        b = self.bank()
                        ps = self.PS[:, b, :]
                        for kc in range(KC):
                            self.mm(ps, wv[:, kc, cc * 128:(cc + 1) * 128], self.HN[:, kc, ts],
                                    kc == 0, kc == KC - 1, [wk, ("HN", kc, t)], [("ps", b)])
                        q = self.tmp_i % 3
                        self.tmp_i += 1
                        tmp = self.TMP[:, q, :]
                        self.act(tmp, ps, AF.Relu, [("ps", b)], [("TMP", q)])
                        self.dve(lambda e, o=A[:, fl, ts], tmp=tmp: e.tensor_tensor(o, tmp, tmp, ALU.mult),
                                 [("TMP", q)], [("A", fl, t)])
            for j2 in range(4):
                r0 = qd * 1024
                wv, wk = self.ring(
                    2048,
                    lambda r0=r0, j2=j2: w_dn[li, r0:r0 + 1024, j2 * 256:(j2 + 1) * 256].rearrange("(k p) f -> p k f", p=128),
                    lambda d: d.rearrange("p (k f) -> p k f", k=8))
                for mm_ in range(2):
                    m = j2 * 2 + mm_
                    for t in range(NT):
                        ts = self.tsl(t)
                        b = self.bank()
                        ps = self.PS[:, b, :]
                        for k in range(8):
                            self.mm(ps, wv[:, k, mm_ * 128:(mm_ + 1) * 128], A[:, k, ts],
                                    k == 0, k == 7, [wk, ("A", k, t)], [("ps", b)])
                        self.dve(lambda e, h=self.H[:, m, ts], ps=ps: e.tensor_tensor(h, ps, h, ALU.add),
                                 [("ps", b), ("H", m, t)], [("H", m, t)])

    def ple(self, li):
        sch = self.sch
        w_g = self.din("ple_w_gate", [DEPTH, D, D])
        w_p = self.din("ple_w_proj", [DEPTH, 256, D])
        pT = self.din("pT", [DEPTH, 256, T])
        sch.dma("pool", "ld_wpj",
                lambda e: e.dma_start(out=self.WPJ[:, :, :], in_=w_p[li].rearrange("(k p) f -> p k f", p=128)),
                writes=[("WPJ",)])
        for jj in range(2):
            wv, wk = self.ring(
                4096,
                lambda jj=jj: w_g[li, :, jj * 512:(jj + 1) * 512].rearrange("(k p) f -> p k f", p=128),
                lambda d: d.rearrange("p (k f) -> p k f", k=8))
            for t in range(NT):
                ts = self.tsl(t)
                pi = self.pt_i % 2
                self.pt_i += 1
                sch.dma("pool", f"ld_pt{pi}",
                        lambda e, pi=pi, ts=ts: e.dma_start(out=self.PT[:, pi, :, :],
                                                            in_=pT[li, :, ts].rearrange("(k p) t -> p k t", p=128)),
                        writes=[("PT", pi)])
                for cc in range(4):
                    m = jj * 4 + cc
                    b = self.bank()
                    ps = self.PS[:, b, :]
                    for kc in range(KC):
                        self.mm(ps, wv[:, kc, cc * 128:(cc + 1) * 128], self.HN[:, kc, ts],
                                kc == 0, kc == KC - 1, [wk, ("HN", kc, t)], [("ps", b)])
                    b2 = self.bank()
                    ps2 = self.PS[:, b2, :]
                    for k2 in range(2):
                        self.mm(ps2, self.WPJ[:, k2, m * 128:(m + 1) * 128], self.PT[:, pi, k2, :],
                                k2 == 0, k2 == 1, [("WPJ",), ("PT", pi)], [("ps", b2)])
                    q = self.tmp_i % 3
                    self.tmp_i += 1
                    g = self.TMP[:, q, :]
                    self.act(g, ps, AF.Sigmoid, [("ps", b)], [("TMP", q)])
                    q2 = self.tmp2_i % 2
                    self.tmp2_i += 1
                    t2 = self.TMP2[:, q2, :]
                    self.dve(lambda e, t2=t2, ps2=ps2, g=g: e.tensor_tensor(t2, ps2, g, ALU.mult),
                             [("ps", b2), ("TMP", q)], [("TMP2", q2)])
                    self.dve(lambda e, h=self.H[:, m, ts], t2=t2: e.tensor_tensor(h, t2, h, ALU.add),
                             [("TMP2", q2), ("H", m, t)], [("H", m, t)])

    def exchange(self, src_ap, dst_ap, ncols, dt, reads, writes, parts_in=None, parts_out=None):
        sch, nc = self.sch, self.nc
        i = self.xch_i
        self.xch_i += 1
        src = nc.dram_tensor(f"xs{i}", [128, ncols], dt)
        gat = nc.dram_tensor(f"xg{i}", [256, ncols], dt)
        ks, kg = (f"xs{i}",), (f"xg{i}",)
        if parts_in is None:
            parts_in = [(src_ap, lambda d: d)]
        if parts_out is None:
            parts_out = [(dst_ap, lambda d: d[0:128, :])]
        sch.dma("pool", "xa",
                [lambda e, a=a, f=f: e.dma_start(out=f(src.ap()), in_=a) for a, f in parts_in],
                reads=reads, writes=[ks])
        sch.dma("pool", "xc",
                lambda e: e.collective_compute("AllGather", ALU.bypass, replica_groups=PAIRS,
                                               ins=[src.ap().opt()], outs=[gat.ap().opt()]),
                reads=[ks], writes=[kg], inc=1)
        sch.dma("pool", "xb",
                [lambda e, a=a, f=f: e.dma_start(out=a, in_=f(gat.ap())) for a, f in parts_out],
                reads=[kg], writes=writes)

    def conv_mixer(self, j):
        sch = self.sch
        w_in = self.din("sc_w_in", [2, D, 3 * D])
        w_out = self.din("sc_w_out", [2, D, D])
        sch.phase(["Z", "ZT", "ZH"])
        ident = self.CM[:, M_ID:M_ID + 128]
        for k in range(3):
            for c in range(KC):
                col = C_SCW + (j * 3 + k) * 8 + c
                dg = self.DIAG[:, k * 8 + c, :]
                self.dve(lambda e, dg=dg, col=col: e.tensor_scalar(dg, ident, self.CF[:, col:col + 1], None, ALU.mult),
                         [("CM",), ("CF",)], [("DIAG", k * 8 + c)])
        ZW = T + 2
        Z = self.S[:, 0:8 * ZW].rearrange("p (k t) -> p k t", k=8)
        ZT = self.SMALL[:, 0:8].bitcast(BF16)
        ZH = self.SMALL[:, 8:16].bitcast(BF16)
        flag = self.CF[:, C_FLAG:C_FLAG + 1]
        w3 = w_in[j].rearrange("(k p) (s f) -> p k s f", p=128, s=3)
        for c in range(KC):
            wv, wk = self.ring(2048, lambda c=c: w3[:, :, 1:3, c * 128:(c + 1) * 128],
                               lambda d: d.rearrange("p (k s f) -> p k s f", k=8, s=2), nsplit=2)
            for t in range(NT):
                ts = self.tsl(t)
                bs = []
                for s_ in range(2):
                    b = self.bank()
                    bs.append(b)
                    for kc in range(KC):
                        self.mm(self.PS[:, b, :], wv[:, kc, s_, :], self.HN[:, kc, ts],
                                kc == 0, kc == KC - 1, [wk, ("HN", kc, t)], [("ps", b)])
                q = self.tmp_i % 3
                self.tmp_i += 1
                tmp = self.TMP[:, q, :]
                self.act(tmp, self.PS[:, bs[0], :], AF.Copy, [("ps", bs[0])], [("TMP", q)])
                self.dve(lambda e, o=Z[:, c, 2 + t * TT:2 + (t + 1) * TT], x=self.PS[:, bs[1], :], tmp=tmp:
                         e.tensor_tensor(o, x, tmp, ALU.mult),
                         [("ps", bs[1]), ("TMP", q)], [("Z", c, t)])
        self.dve(lambda e: e.tensor_copy(ZT.rearrange("p (c w) -> p c w", c=8), Z[:, :, T:T + 2]),
                 [("Z", c, NT - 1) for c in range(KC)], [("ZT",)])
        self.exchange(ZT, ZH, 16, BF16, [("ZT",)], [("ZH",)])
        self.dve(lambda e: e.tensor_scalar(Z[:, :, 0:2], ZH.rearrange("p (c w) -> p c w", c=8), flag, None, ALU.mult),
                 [("ZH",), ("CF",)], [("Z", c, "h") for c in range(KC)])
        gb = []
        for hh in range(2):
            gb.append(self.ring(4096, lambda hh=hh: w_in[j, :, hh * 512:(hh + 1) * 512].rearrange("(k p) f -> p k f", p=128),
                                lambda d: d.rearrange("p (k f) -> p k f", k=8)))
        for tgroup in ((3, 2, 1), (0,)):
            for c in range(KC):
                wv, wk = gb[c // 4]
                cl = c % 4
                for t in tgroup:
                    ts = self.tsl(t)
                    bg = self.bank()
                    for kc in range(KC):
                        self.mm(self.PS[:, bg, :], wv[:, kc, cl * 128:(cl + 1) * 128], self.HN[:, kc, ts],
                                kc == 0, kc == KC - 1, [wk, ("HN", kc, t)], [("ps", bg)])
                    bc = self.bank()
                    prev = ("Z", c, t - 1) if t > 0 else ("Z", c, "h")
                    for k in range(3):
                        self.mm(self.PS[:, bc, :], self.DIAG[:, k * 8 + c, :], Z[:, c, t * TT + k:t * TT + k + TT],
                                k == 0, k == 2, [("DIAG", k * 8 + c), ("Z", c, t), prev], [("ps", bc)])
                    q = self.tmp_i % 3
                    self.tmp_i += 1
                    tmp = self.TMP[:, q, :]
                    self.act(tmp, self.PS[:, bg, :], AF.Copy, [("ps", bg)], [("TMP", q)])
                    self.dve(lambda e, o=Z[:, c, 2 + t * TT:2 + (t + 1) * TT], x=self.PS[:, bc, :], tmp=tmp:
                             e.tensor_tensor(o, x, tmp, ALU.mult),
                             [("ps", bc), ("TMP", q)], [("Z", c, t)])
        for jj in range(2):
            wv, wk = self.ring(4096, lambda jj=jj: w_out[j, :, jj * 512:(jj + 1) * 512].rearrange("(k p) f -> p k f", p=128),
                               lambda d: d.rearrange("p (k f) -> p k f", k=8))
            for cc in range(4):
                m = jj * 4 + cc
                for t in range(NT):
                    ts = self.tsl(t)
                    b = self.bank()
                    ps = self.PS[:, b, :]
                    for kc in range(KC):
                        self.mm(ps, wv[:, kc, cc * 128:(cc + 1) * 128], Z[:, kc, 2 + t * TT:2 + (t + 1) * TT],
                                kc == 0, kc == KC - 1, [wk, ("Z", kc, t)], [("ps", b)])
                    self.dve(lambda e, h=self.H[:, m, ts], ps=ps: e.tensor_tensor(h, ps, h, ALU.add),
                             [("ps", b), ("H", m, t)], [("H", m, t)])

    def lru_mixer(self, j):
        sch = self.sch
        w_in = self.din("lru_w_in", [1, D, 2560])
        w_a = self.din("lru_w_a", [1, 10, 128, 128])
        w_x = self.din("lru_w_x", [1, 10, 128, 128])
        w_out = self.din("lru_w_out", [1, 1280, D])
        sch.phase(["G", "XR", "XB", "RA", "I", "M"])
        S = self.S
        ident = self.CM[:, M_ID:M_ID + 128]
        flag = self.CF[:, C_FLAG:C_FLAG + 1]
        SM = self.SMALL
        XT = SM[:, 16:36].bitcast(BF16)
        XH = SM[:, 36:56].bitcast(BF16)
        ONEC = SM[:, 61:62]
        HS = SM[:, 62:63]
        H0 = SM[:, 63:64]
        CN = SM[:, 0:10]
        self.dve(lambda e: e.memset(ONEC, 1.0), [], [("ONEC",)])
        for k in range(4):
            for n in range(10):
                col = C_LCW + k * 10 + n
                dg = self.DIAG[:, k * 10 + n, :]
                self.dve(lambda e, dg=dg, col=col: e.tensor_scalar(dg, ident, self.CF[:, col:col + 1], None, ALU.mult),
                         [("CM",), ("CF",)], [("DIAG", k * 10 + n)])
        self.act(CN, self.CF[:, C_LAM:C_LAM + 10], AF.Exp, [("CF",)], [("CN",)], scale=-1.0)
        self.act(CN, CN, AF.Ln, [("CN",), ("ONEC",)], [("CN",)], bias=ONEC)
        self.dve(lambda e: e.tensor_scalar(CN, CN, -8.0, None, ALU.mult), [("CN",)], [("CN",)])
        WA = self.PT[:, :, :, :].rearrange("p a b t -> p (a b t)")[:, 0:1280].rearrange("p (n k) -> p n k", n=10)
        WX = self.WPJ[:, :, :].rearrange("p a f -> p (a f)")[:, 0:1280].rearrange("p (n k) -> p n k", n=10)
        sch.dma("pool", "ld_pt0", lambda e: e.dma_start(out=WA, in_=w_a[0].rearrange("n j k -> j n k")),
                writes=[("PT", 0), ("PT", 1)])
        sch.dma("pool", "ld_wpj", lambda e: e.dma_start(out=WX, in_=w_x[0].rearrange("n j k -> j n k")),
                writes=[("WPJ",)])
        bh = self.bank(aux=True)
        for (c0, cw) in ((1280, 512), (1792, 512), (2304, 256)):
            wv, wk = self.ring(8 * cw, lambda c0=c0, cw=cw: w_in[0, :, c0:c0 + cw].rearrange("(k p) f -> p k f", p=128),
                               lambda d: d.rearrange("p (k f) -> p k f", k=8))
            for cl in range(cw // 128):
                n = (c0 - 1280) // 128 + cl
                for kc in range(KC):
                    self.mm(self.PS[:, bh, n * 4:(n + 1) * 4], wv[:, kc, cl * 128:(cl + 1) * 128], self.HN[:, kc, T - 4:T],
                            kc == 0, kc == KC - 1, [wk, ("HN", kc, NT - 1)], [("ps", bh)])
        self.act(XT, self.PS[:, bh, 0:40], AF.Copy, [("ps", bh)], [("XT",)])
        self.exchange(XT, XH, 40, BF16, [("XT",)], [("XH",)])
        w2 = w_in[0].rearrange("(k p) (s f) -> p k s f", p=128, s=2)
        Gs = [S[:, 0:2048], S[:, 2048:4096]]
        XRs = [S[:, 4096:6148], S[:, 6152:8204]]
        XBs = [S[:, 8208:10256], S[:, 10256:12304]]
        RA = S[:, 12304:16400].bitcast(F32)
        II = S[:, 16400:20496].bitcast(F32)
        MM = S[:, 20496:24592].bitcast(F32)

        def kk(p, par, t):
            return (p, par, t)

        def front(n):
            par = n % 2
            G, XR, XB = Gs[par], XRs[par], XBs[par]
            wv, wk = self.ring(2048, lambda n=n: w2[:, :, :, n * 128:(n + 1) * 128],
                               lambda d: d.rearrange("p (k s f) -> p k s f", k=8, s=2), nsplit=2)
            for t in range(NT):
                ts = self.tsl(t)
                bg = self.bank()
                for kc in range(KC):
                    self.mm(self.PS[:, bg, :], wv[:, kc, 0, :], self.HN[:, kc, ts], kc == 0, kc == KC - 1,
                            [wk, ("HN", kc, t)], [("ps", bg)])
                self.act(G[:, ts], self.PS[:, bg, :], AF.Gelu, [("ps", bg)], [kk("G", par, t)])
                bx = self.bank()
                for kc in range(KC):
                    self.mm(self.PS[:, bx, :], wv[:, kc, 1, :], self.HN[:, kc, ts], kc == 0, kc == KC - 1,
                            [wk, ("HN", kc, t)], [("ps", bx)])
                self.act(XR[:, 4 + t * TT:4 + (t + 1) * TT], self.PS[:, bx, :], AF.Copy, [("ps", bx)], [kk("XR", par, t)])
            self.dve(lambda e, n=n, XR=XR: e.tensor_scalar(XR[:, 0:4], XH[:, n * 4:(n + 1) * 4], flag, None, ALU.mult),
                     [("XH",), ("CF",)], [kk("XR", par, "h")])
            for t in range(NT):
                ts = self.tsl(t)
                bc = self.bank()
                prev = kk("XR", par, t - 1) if t > 0 else kk("XR", par, "h")
                for k in range(4):
                    self.mm(self.PS[:, bc, :], self.DIAG[:, k * 10 + n, :], XR[:, 1 + t * TT + k:1 + t * TT + k + TT],
                            k == 0, k == 3, [("DIAG", k * 10 + n), kk("XR", par, t), prev], [("ps", bc)])
                self.act(XB[:, ts], self.PS[:, bc, :], AF.Identity, [("ps", bc), ("CF",)], [kk("XB", par, t)],
                         bias=self.CF[:, C_LCB + n:C_LCB + n + 1])
                ba = self.bank()
                self.mm(self.PS[:, ba, :], WA[:, n, :], XB[:, ts], True, True, [("PT", 0), kk("XB", par, t)], [("ps", ba)])
                self.act(RA[:, ts], self.PS[:, ba, :], AF.Sigmoid, [("ps", ba), ("CF",)], [("RA", t)],
                         bias=self.CF[:, C_LBA + n:C_LBA + n + 1])
                bx2 = self.bank()
                self.mm(self.PS[:, bx2, :], WX[:, n, :], XB[:, ts], True, True, [("WPJ",), kk("XB", par, t)], [("ps", bx2)])
                self.act(II[:, ts], self.PS[:, bx2, :], AF.Sigmoid, [("ps", bx2), ("CF",)], [("I", t)],
                         bias=self.CF[:, C_LBX + n:C_LBX + n + 1])

        def mid(n):
            par = n % 2
            G, XB = Gs[par], XBs[par]
            allt = lambda p: [(p, t) for t in range(NT)]
            allp = lambda p: [(p, par, t) for t in range(NT)]
            self.act(RA, RA, AF.Exp, allt("RA") + [("CN",)], allt("RA"), scale=CN[:, n:n + 1])
            self.dve(lambda e: e.tensor_tensor(MM, RA, RA, ALU.mult), allt("RA"), [("M",)])
            self.act(MM, MM, AF.Sqrt, [("M",), ("ONEC",)], [("M",)], bias=ONEC, scale=-1.0)
            self.dve(lambda e, XB=XB: e.tensor_tensor(II, II, XB, ALU.mult), allt("I") + allp("XB"), allt("I"))
            self.dve(lambda e: e.tensor_tensor(II, II, MM, ALU.mult), allt("I") + [("M",)], allt("I"))
            self.dve(lambda e: e.tensor_tensor_scan(MM, RA, II, 0.0, ALU.mult, ALU.add),
                     allt("RA") + allt("I") + [("M",)], [("M",)])
            self.dve(lambda e: e.tensor_copy(HS, MM[:, T - 1:T]), [("M",)], [("HS",)])
            self.dve(lambda e: e.tensor_scalar(II, RA, 0.0, None, ALU.mult), allt("RA") + allt("I"), allt("I"))
            self.dve(lambda e: e.tensor_tensor_scan(II, RA, II, 1.0, ALU.mult, ALU.add),
                     allt("RA") + allt("I"), allt("I"))
            self.exchange(HS, H0, 1, F32, [("HS",)], [("H0",)])
            self.dve(lambda e, XB=XB, G=G: e.tensor_tensor(XB, II, G, ALU.mult), allt("I") + allp("G") + allp("XB"), allp("XB"))
            self.dve(lambda e, G=G: e.tensor_tensor(G, MM, G, ALU.mult), [("M",)] + allp("G"), allp("G"))

        def back(n):
            par = n % 2
            G, XB = Gs[par], XBs[par]
            allp = lambda p: [(p, par, t) for t in range(NT)]
            self.dve(lambda e: e.tensor_scalar(H0, H0, flag, None, ALU.mult), [("H0",), ("CF",)], [("H0",)])
            self.dve(lambda e, XB=XB, G=G: e.scalar_tensor_tensor(G, XB, H0, G, ALU.mult, ALU.add),
                     allp("XB") + allp("G") + [("H0",)], allp("G"))
            wo, wok = self.ring(1024, lambda n=n: w_out[0, n * 128:(n + 1) * 128, :], lambda d: d)
            for m in range(KC):
                for t in range(NT):
                    ts = self.tsl(t)
                    b = self.bank()
                    self.mm(self.PS[:, b, :], wo[:, m * 128:(m + 1) * 128], G[:, ts], True, True,
                            [wok, ("G", par, t)], [("ps", b)])
                    self.dve(lambda e, h=self.H[:, m, ts], ps=self.PS[:, b, :]: e.tensor_tensor(h, ps, h, ALU.add),
                             [("ps", b), ("H", m, t)], [("H", m, t)])

        if LRU_PIPE:
            front(0)
            for n in range(10):
                mid(n)
                if n + 1 < 10:
                    front(n + 1)
                back(n)
        else:
            for n in range(10):
                front(n)
                mid(n)
                back(n)

    def rope_tables(self):
        S = self.S
        pos = self.din("pos", [32, T], I32)
        PI = S[0:32, 0:4096].bitcast(I32)
        ANG = S[0:32, 4096:8192].bitcast(F32)
        KF = S[0:32, 8192:12288].bitcast(F32)
        ANC = S[0:32, 12288:16384].bitcast(F32)
        COS = self.TMP2[0:32, :, :].rearrange("p a t -> p (a t)").bitcast(BF16)
        SIN = self.RSTD[0:32, :, :].rearrange("p a t -> p (a t)").bitcast(BF16)
        tk = [("TMP2", 0), ("TMP2", 1)]
        rk = [("RSTD", 0), ("RSTD", 1)]
        self.sch.dma("sp", "ld_pos", lambda e: e.dma_start(out=PI, in_=pos), writes=[("RT",)])
        self.dve(lambda e: e.tensor_copy(ANG, PI), [("RT",)], [("RT",)])
        self.dve(lambda e: e.tensor_scalar(ANG, ANG, self.CF[0:32, C_INVF:C_INVF + 1], None, ALU.mult),
                 [("RT",), ("CF",)], [("RT",)])
        MAGIC = 12582912.0
        TWO_PI = 6.283185307179586
        PI_ = 3.1415925
        for which, dst, dk in ((0, SIN, rk), (1, COS, tk)):
            src = ANG
            if which == 1:
                self.dve(lambda e: e.tensor_scalar(ANC, ANG, 1.5707963267948966, None, ALU.add), [("RT",)], [("RT",)])
                src = ANC
            self.dve(lambda e, src=src: e.tensor_scalar(KF, src, 1.0 / TWO_PI, MAGIC, ALU.mult, ALU.add),
                     [("RT",), ("RT",)], [("RT",)])
            self.dve(lambda e: e.tensor_scalar(KF, KF, -MAGIC, -TWO_PI, ALU.add, ALU.mult), [("RT",)], [("RT",)])
            self.dve(lambda e, src=src: e.tensor_tensor(KF, KF, src, ALU.add), [("RT",), ("RT",), ("RT",)], [("RT",)])
            self.dve(lambda e: e.tensor_scalar(KF, KF, -PI_, PI_, ALU.max, ALU.min), [("RT",)], [("RT",)])
            self.act(dst, KF, AF.Sin, [("RT",)], dk)
        return COS, SIN, tk, rk

    def attn_mixer(self, j):
        sch = self.sch
        w_qkv = self.din("attn_w_qkv", [1, D, 9216])
        w_o = self.din("attn_w_o", [1, D, D])
        sch.phase(["RT"])
        S = self.S
        COS, SIN, tk, rk = self.rope_tables()
        sch.phase(["KT", "V", "QT", "VT", "ND"])
        KT = S[:, 0:6144].rearrange("p (g t) -> p g t", g=3)
        V = S[:, 6144:12288].rearrange("p (b d) -> p b d", d=128)
        QT = S[:, 12288:14336]
        VT = S[:, 14336:16384]
        ND = S[:, 16384:24576].bitcast(F32).rearrange("p (a t) -> p a t", a=2)
        HK = self.DIAG[:, 0:21, :]
        HVa = self.WPJ[:, :, :].rearrange("p a (b d) -> p (a b) d", d=128)
        HVb = self.PT[:, :, :, :].rearrange("p a b t -> p (a b t)")[:, 0:640].rearrange("p (b d) -> p b d", d=128)
        ident = self.CM[:, M_ID:M_ID + 128]
        ones = self.CM[:, M_ONES:M_ONES + 128]
        rot = self.CM[:, M_ROT128:M_ROT128 + 128]
        RT = [self.TMP[:, i, :].bitcast(BF16) for i in range(3)] + \
             [self.DIAG[:, 21 + 8 * i:29 + 8 * i, :].rearrange("p a b -> p (a b)") for i in range(2)]
        dummy = self.SMALL[:, 59:60]
        self.dve(lambda e: e.memset(dummy, 0.0), [], [("SQ", i) for i in range(3)] + [("P", i) for i in range(6)])
        for i in range(3):
            self.dve(lambda e, i=i: e.memset(RT[i], 0.0), [("TMP", i)], [("RTB", i), ("TMP", i)])
        for i in range(2):
            self.dve(lambda e, i=i: e.memset(RT[3 + i], 0.0), [("DIAG", 21 + 8 * i + a) for a in range(8)],
                     [("RTB", 3 + i)] + [("DIAG", 21 + 8 * i + a) for a in range(8)])
        DIL = (1, 4, 16)
        SCALE = 128.0 ** -0.5
        wq4 = w_qkv[0].rearrange("(k p) (s g h f) -> p k s g h f", p=128, s=3, g=3, h=8)

        def hv(i):
            return (HVa[:, i, :], ("WPJ",)) if i < 16 else (HVb[:, i - 16, :], ("PT", 0))

        def perm_out(dst2d, t, d):
            v = dst2d.rearrange("p (r m) -> p m r", r=d)
            return v[:, t * (TT // d):(t + 1) * (TT // d), :]

        def project(wslice, wk, dst2d, dkey, d, rope):
            for t in range(NT):
                ts = self.tsl(t)
                b = self.bank()
                ps = self.PS[:, b, :]
                for kc in range(KC):
                    self.mm(ps, wslice[:, kc, :], self.HN[:, kc, ts], kc == 0, kc == KC - 1,
                            [wk, ("HN", kc, t)], [("ps", b)])
                pin = ps.rearrange("p (m r) -> p m r", r=d)
                self.act(perm_out(dst2d, t, d), pin, AF.Copy, [("ps", b)], [dkey])
                if rope:
                    qi = self.rt_i % 5; self.rt_i += 1
                    U = RT[qi]
                    self.dve(lambda e, U=U, ps=ps, ts=ts: e.tensor_tensor(U[0:32, 0:TT], ps[0:32, :], COS[:, ts], ALU.mult),
                             [("ps", b)] + tk, [("RTB", qi)])
                    self.dve(lambda e, U=U, ps=ps, ts=ts: e.tensor_tensor(U[0:32, TT:2 * TT], ps[0:32, :], SIN[:, ts], ALU.mult),
                             [("ps", b)] + rk, [("RTB", qi)])
                    b2 = self.bank()
                    self.mm(self.PS[:, b2, :], ident, U[:, 0:TT], True, False, [("RTB", qi), ("CM",)], [("ps", b2)])
                    self.mm(self.PS[:, b2, :], rot, U[:, TT:2 * TT], False, True, [("RTB", qi), ("CM",)], [("ps", b2)])
                    self.act(perm_out(dst2d[0:32, :], t, d), self.PS[0:32, b2, :].rearrange("p (m r) -> p m r", r=d),
                             AF.Copy, [("ps", b2), dkey], [dkey])

        if ATT_STAGE == 0:
            return
        for h in range(8 if ATT_STAGE == 9 else 1):
            for g in range(3):
                d = DIL[g]
                wv, wk = self.ring(2048, lambda g=g, h=h: wq4[:, :, 1:3, g, h, :],
                                   lambda dd: dd.rearrange("p (k s f) -> p k s f", k=8, s=2), nsplit=2)
                project(wv[:, :, 0, :], wk, KT[:, g, :], ("KT", g), d, ATT_STAGE >= 0.7)
                project(wv[:, :, 1, :], wk, VT, ("VT",), d, False)
                for bq in range(4 if ATT_STAGE >= 1 else 0):
                    b = self.bank()
                    psb = self.PS[:, b, 0:256].bitcast(BF16)
                    for i in range(4):
                        blk = bq * 4 + i
                        self.sch.op("pe", lambda e, o=psb[:, i * 128:(i + 1) * 128], a=VT[:, blk * 128:(blk + 1) * 128]:
                                    e.transpose(o, a, ident), [("VT",), ("CM",)], [("ps", b)])
                    self.act(V[:, g * 16 + bq * 4:g * 16 + bq * 4 + 4, :],
                             psb.rearrange("p (i d) -> p i d", d=128), AF.Copy, [("ps", b)], [("V", g)])
            qw = [self.ring(1024, lambda g=g, h=h: wq4[:, :, 0, g, h, :],
                            lambda dd: dd.rearrange("p (k f) -> p k f", k=8)) for g in range(3)]
            kt1 = KT[:, 1, :].rearrange("p (r b c) -> p r b c", r=4, b=4)[:, :, 3, :]
            v1 = V[:, 16:32, :].rearrange("p (r b) d -> p r b d", r=4)[:, :, 3, :]
            NB = 128
            parts_in = [
                (KT[:, 0, 15 * NB:16 * NB], lambda dd: dd[:, 0:NB]),
                (kt1, lambda dd: dd[:, NB:5 * NB].rearrange("p (r c) -> p r c", r=4)),
                (KT[:, 2, :], lambda dd: dd[:, 5 * NB:21 * NB]),
                (V[:, 15, :], lambda dd: dd[:, 21 * NB:22 * NB]),
                (v1, lambda dd: dd[:, 22 * NB:26 * NB].rearrange("p (r c) -> p r c", r=4)),
                (V[:, 32:48, :], lambda dd: dd[:, 26 * NB:42 * NB].rearrange("p (b c) -> p b c", b=16)),
            ]
            parts_out = [
                (HK, lambda gg: gg[0:128, 0:21 * NB].rearrange("p (b c) -> p b c", b=21)),
                (HVa, lambda gg: gg[0:128, 21 * NB:37 * NB].rearrange("p (b c) -> p b c", b=16)),
                (HVb, lambda gg: gg[0:128, 37 * NB:42 * NB].rearrange("p (b c) -> p b c", b=5)),
            ]
            self.exchange(None, None, 42 * NB, BF16,
                          [("KT", 0), ("KT", 1), ("KT", 2), ("V", 0), ("V", 1), ("V", 2)],
                          [("DIAG", i) for i in range(21)] + [("WPJ",), ("PT", 0), ("PT", 1)],
                          parts_in=parts_in, parts_out=parts_out)
            for g in range(3):
                d = DIL[g]
                nb = 16 // d
                wv, wk = qw[g]
                project(wv, wk, QT, ("QT",), d, True)
                order = [x for x in range(16) if x % nb != 0] + [x for x in range(16) if x % nb == 0]
                pend = {}

                def stage_s(jb, g=g, d=d, nb=nb):
                    r, jl = jb // nb, jb % nb
                    first = jl == 0
                    qs = QT[:, jb * 128:(jb + 1) * 128]
                    if first:
                        hb = (0, 1 + r, 5 + r)[g]
                        kprev, kpk = HK[:, hb, :], ("DIAG", hb)
                        vprev, vpk = hv(hb)
                        mask = self.CM[:, M_MASKH:M_MASKH + 256]
                    else:
                        kprev, kpk = KT[:, g, (jb - 1) * 128:jb * 128], ("KT", g)
                        vprev, vpk = V[:, g * 16 + jb - 1, :], ("V", g)
                        mask = self.CM[:, M_MASK:M_MASK + 256]
                    b = self.bank()
                    ps = self.PS[:, b, 0:256]
                    self.mm(ps[:, 0:128], kprev, qs, True, True, [kpk, ("QT",)], [("ps", b)])
                    self.mm(ps[:, 128:256], KT[:, g, jb * 128:(jb + 1) * 128], qs, True, True,
                            [("KT", g), ("QT",)], [("ps", b)])
                    q = self.p_i % 6; self.p_i += 1
                    P = self.SQ[:, q // 2, (q % 2) * 256:(q % 2) * 256 + 256]
                    pk = ("P", q)
                    self.act(P, ps, AF.Exp, [("ps", b), ("SQ", q // 2)], [pk], scale=SCALE)
                    self.dve(lambda e, P=P, mask=mask: e.tensor_tensor(P, P, mask, ALU.mult),
                             [pk, ("CM",)], [pk])
                    pend[jb] = (P, pk, vprev, vpk, r, jl)

                def stage_pv(jb, g=g, d=d):
                    P, pk, vprev, vpk, r, jl = pend.pop(jb)
                    b2 = self.bank()
                    po = self.PS[:, b2, 0:256]
                    self.mm(po[:, 0:128], vprev, P[:, 0:128], True, False, [vpk, pk], [("ps", b2)])
                    self.mm(po[:, 0:128], V[:, g * 16 + jb, :], P[:, 128:256], False, True, [("V", g), pk], [("ps", b2)])
                    self.mm(po[:, 128:256], ones, P[:, 0:128], True, False, [("CM",), pk], [("ps", b2)])
                    self.mm(po[:, 128:256], ones, P[:, 128:256], False, True, [("CM",), pk], [("ps", b2)])
                    st = jl * 128 * d + r
                    dst = ND[:, :, st:st + 127 * d + 1:d]
                    pin = po.rearrange("p (a c) -> p a c", a=2)
                    if g == 0:
                        self.dve(lambda e, dst=dst, pin=pin: e.tensor_copy(dst, pin), [("ps", b2)], [("ND",)])
                    else:
                        self.dve(lambda e, dst=dst, pin=pin: e.tensor_tensor(dst, pin, dst, ALU.add),
                                 [("ps", b2), ("ND",)], [("ND",)])

                LOOK = BLOCK_LOOKAHEAD
                for i, jb in enumerate(order):
                    stage_s(jb)
                    if i >= LOOK:
                        stage_pv(order[i - LOOK])
                for jb in order[len(order) - LOOK:] if LOOK else []:
                    stage_pv(jb)
            self.dve(lambda e: e.reciprocal(ND[:, 1, :], ND[:, 1, :]), [("ND",)], [("ND",)])
            self.dve(lambda e: e.tensor_tensor(VT, ND[:, 0, :], ND[:, 1, :], ALU.mult), [("ND",), ("VT",)], [("VT",)])
            wo, wok = self.ring(1024, lambda h=h: w_o[0, h * 128:(h + 1) * 128, :], lambda dd: dd)
            for m in range(KC):
                for t in range(NT):
                    ts = self.tsl(t)
                    b = self.bank()
                    self.mm(self.PS[:, b, :], wo[:, m * 128:(m + 1) * 128], VT[:, ts], True, True,
                            [wok, ("VT",)], [("ps", b)])
                    self.dve(lambda e, hh=self.H[:, m, ts], ps=self.PS[:, b, :]: e.tensor_tensor(hh, ps, hh, ALU.add),
                             [("ps", b), ("H", m, t)], [("H", m, t)])
        self.dve(lambda e: e.memset(dummy, 0.0), [],
                 [("SQ", i) for i in range(3)] + [("P", i) for i in range(6)] + [("RTB", i) for i in range(5)] +
                 [("TMP", i) for i in range(3)] + [("DIAG", 21 + a) for a in range(16)])


_CACHE = {}


def _const_tables(flag):
    cm = np.zeros((128, NCM), np.float32)
    cm[:, M_ID:M_ID + 128] = np.eye(128, dtype=np.float32)
    cm[:, M_ONESD:M_ONESD + 128] = 1.0 / 1024.0
    cm[:, M_ONES:M_ONES + 128] = 1.0
    for m in range(16):
        cm[m + 16, M_ROT + m] = -1.0
        cm[m, M_ROT + 16 + m] = 1.0
        cm[m + 16, M_ROT128 + m] = -1.0
        cm[m, M_ROT128 + 16 + m] = 1.0
    k = np.arange(128)[:, None]
    q = np.arange(128)[None, :]
    maskP = (k >= q).astype(np.float32)
    maskO = (k <= q).astype(np.float32)
    cm[:, M_MASK:M_MASK + 128] = maskP
    cm[:, M_MASK + 128:M_MASK + 256] = maskO
    cm[:, M_MASKH:M_MASKH + 128] = maskP * flag
    cm[:, M_MASKH + 128:M_MASKH + 256] = maskO
    return cm


def _cols(v):
    v = np.asarray(v, np.float32)
    return np.ascontiguousarray(v.reshape(-1, 128).T)


def _cf_table(inp, flag):
    cf = np.zeros((128, NCF), np.float32)
    for l in range(DEPTH):
        cf[:, C_NMIX + l * 8:C_NMIX + l * 8 + 8] = _cols(inp["norm_mix"][l])
        cf[:, C_NMLP + l * 8:C_NMLP + l * 8 + 8] = _cols(inp["norm_mlp"][l])
        cf[:, C_NPLE + l * 8:C_NPLE + l * 8 + 8] = _cols(inp["norm_ple"][l])
    cf[:, C_NFIN:C_NFIN + 8] = _cols(inp["norm_final"])
    for jj in range(2):
        for k in range(3):
            c0 = C_SCW + (jj * 3 + k) * 8
            cf[:, c0:c0 + 8] = _cols(inp["sc_w_conv"][jj, k])
    for k in range(4):
        cf[:, C_LCW + k * 10:C_LCW + k * 10 + 10] = _cols(inp["lru_conv_w"][0, k])
    cf[:, C_LCB:C_LCB + 10] = _cols(inp["lru_conv_b"][0])
    cf[:, C_LBA:C_LBA + 10] = _cols(inp["lru_b_a"][0])
    cf[:, C_LBX:C_LBX + 10] = _cols(inp["lru_b_x"][0])
    cf[:, C_LAM:C_LAM + 10] = _cols(inp["lru_lambda"][0])
    cf[:, C_FLAG] = flag
    half = 16
    inv = (500000.0 ** (-2.0 * np.arange(half, dtype=np.float32) / 32.0)).astype(np.float32)
    cf[0:16, C_INVF] = inv
    cf[16:32, C_INVF] = inv
    return cf


WEIGHT_NAMES = ["sc_w_in", "sc_w_out", "attn_w_qkv", "attn_w_o", "lru_w_in", "lru_w_a", "lru_w_x",
                "lru_w_out", "mlp_w_up", "mlp_w_down", "ple_w_gate", "ple_w_proj"]


def run_layers(inp, hT_per_core, layers, final_norm):
    key = (tuple(layers), final_norm)
    if key not in _CACHE:
        prog = Prog(list(layers), final_norm)
        nc = prog.build()
        _CACHE[key] = (prog, nc)
    prog, nc = _CACHE[key]
    in_maps = []
    for c in range(N_CORES):
        b, half = c // 2, c % 2
        tok = slice(half * T, (half + 1) * T)
        m = {}
        for name in prog.in_names:
            if name == "cf":
                m[name] = _cf_table(inp, float(half))
            elif name == "cm":
                m[name] = _const_tables(float(half))
            elif name == "xT":
                m[name] = np.ascontiguousarray(hT_per_core[c], dtype=np.float32)
            elif name == "pT":
                m[name] = np.ascontiguousarray(np.transpose(inp["p"][:, b, tok, :], (0, 2, 1)), dtype=np.float32)
            elif name == "pos":
                m[name] = np.ascontiguousarray(np.broadcast_to(inp["positions"][b, tok].reshape(1, T), (32, T)), dtype=np.int32)
            else:
                m[name] = np.ascontiguousarray(inp[name], dtype=np.float32)
        in_maps.append(m)
    res = run_bass_kernel_spmd(nc, in_maps, core_ids=list(range(N_CORES)))
    return [res.results[c]["outT"] for c in range(N_CORES)]


LAUNCH_PLAN = [([0, 1, 2, 3], True)]


def kernel(**inputs):
    inp = {k: np.asarray(v) for k, v in inputs.items()}
    x = inp["x"]
    hT = []
    for c in range(N_CORES):
        b, half = c // 2, c % 2
        hT.append(np.ascontiguousarray(x[b, half * T:(half + 1) * T, :].T))
    for layers, fin in LAUNCH_PLAN:
        hT = run_layers(inp, hT, layers, fin)
    out = np.empty((4, 2 * T, D), np.float32)
    for c in range(N_CORES):
        b, half = c // 2, c % 2
        out[b, half * T:(half + 1) * T, :] = hT[c].T
    return out
```
